# Optimizing a Trainium2 kernel written in Bass

```python
import math
import jax, jax.numpy as jnp
from jax import lax
import numpy as np

D_MODEL = 4096
BATCH = 2
SEQ = 8192
DEPTH = 1

ATTN_GROUPS = ((128, 1), (512, 4), (2048, 16))
N_GROUPS = 3
ATTN_HEAD_DIM = 128
ATTN_HEADS_PER_GROUP = D_MODEL // 512
ATTN_BLOCK = 128
ATTN_QKV_WIDTH = N_GROUPS * ATTN_HEADS_PER_GROUP * ATTN_HEAD_DIM
ATTN_OUT_WIDTH = ATTN_HEADS_PER_GROUP * ATTN_HEAD_DIM
MLSTM_HEADS = 8
MLSTM_QK_DIM = D_MODEL // 16
MLSTM_V_DIM = D_MODEL // 8
MLSTM_QK_WIDTH = MLSTM_HEADS * MLSTM_QK_DIM
MLSTM_V_WIDTH = MLSTM_HEADS * MLSTM_V_DIM
MLSTM_CHUNK = 64
GATE_SOFTCAP = 15.0
D_FF = 4 * D_MODEL
RMS_EPS = 1e-6
NEG = -1e30
IN_SPLITS = (ATTN_QKV_WIDTH, ATTN_QKV_WIDTH, ATTN_QKV_WIDTH,
             MLSTM_QK_WIDTH, MLSTM_QK_WIDTH, MLSTM_V_WIDTH, MLSTM_V_WIDTH,
             MLSTM_HEADS, MLSTM_HEADS)
IN_WIDTH = sum(IN_SPLITS)

kernel_name = 'hybrid_dilated_attn_mlstm_block'


def rms_norm(x, g):
    x32 = x.astype(jnp.float32)
    y = x32 * lax.rsqrt(jnp.mean(x32 * x32, axis=-1, keepdims=True) + RMS_EPS)
    return (y * g.astype(jnp.float32)).astype(x.dtype)


def _alibi_slope_list(n):
    def pow2(m):
        start = 2.0 ** (-(2.0 ** -(math.log2(m) - 3)))
        return [start ** (i + 1) for i in range(m)]
    if math.log2(n).is_integer():
        return pow2(n)
    c = 2 ** math.floor(math.log2(n))
    return pow2(c) + _alibi_slope_list(2 * c)[0::2][: n - c]


def attn_alibi_slopes():
    s = sorted(_alibi_slope_list(N_GROUPS * ATTN_HEADS_PER_GROUP), reverse=True)
    return np.asarray(s, dtype=np.float32).reshape(N_GROUPS, ATTN_HEADS_PER_GROUP)


def dilated_window_attention(q, k, v, window, dilation, slopes):
    B, S, H, E = q.shape
    d = dilation
    w = window // d
    L = S // d
    nb = -(-L // ATTN_BLOCK)
    Lp = nb * ATTN_BLOCK

    def strided_blocks(t):
        t = t.astype(jnp.float32).reshape(B, L, d, H, E).transpose(0, 3, 2, 1, 4)
        t = jnp.pad(t, ((0, 0), (0, 0), (0, 0), (0, Lp - L), (0, 0)))
        return t.reshape(B, H, d, nb, ATTN_BLOCK, E)

    def with_prev_block(t):
        prev = jnp.pad(t, ((0, 0), (0, 0), (0, 0), (1, 0), (0, 0), (0, 0)))[:, :, :, :-1]
        return jnp.concatenate([prev, t], axis=4)

    qb = strided_blocks(q)
    kk = with_prev_block(strided_blocks(k))
    vv = with_prev_block(strided_blocks(v))
    scores = jnp.einsum('bhrnqe,bhrnke->bhrnqk', qb, kk) * (E ** -0.5)
    qi = jnp.arange(ATTN_BLOCK)
    ki = jnp.arange(2 * ATTN_BLOCK) - ATTN_BLOCK
    rel = qi[:, None] - ki[None, :]
    key_pos = (jnp.arange(nb) * ATTN_BLOCK)[:, None] + ki[None, :]
    valid = ((rel >= 0) & (rel <= w))[None] & (key_pos >= 0)[:, None, :]
    bias = -slopes[:, None, None] * (rel * d).astype(jnp.float32)[None]
    scores = scores + bias[:, None, None]
    scores = jnp.where(valid, scores, NEG)
    lse = jax.nn.logsumexp(scores, axis=-1)
    p = jnp.exp(scores - lse[..., None])
    out = jnp.einsum('bhrnqk,bhrnke->bhrnqe', p, vv)
    out = out.reshape(B, H, d, Lp, E)[:, :, :, :L].transpose(0, 3, 2, 1, 4).reshape(B, S, H, E)
    lse = lse.reshape(B, H, d, Lp)[..., :L].transpose(0, 3, 2, 1).reshape(B, S, H)
    return out, lse


def mlstm_chunkwise(q, k, v, log_i, log_f):
    B, S, H, K = q.shape
    V = v.shape[-1]
    Lc = MLSTM_CHUNK
    NC = S // Lc

    def vec_chunks(t):
        return t.astype(jnp.float32).reshape(B, NC, Lc, H, t.shape[-1]).transpose(1, 0, 3, 2, 4)

    def gate_chunks(t):
        return t.astype(jnp.float32).reshape(B, NC, Lc, H).transpose(1, 0, 3, 2)

    causal = jnp.tril(jnp.ones((Lc, Lc), dtype=bool))

    def step(carry, xs):
        C, n, m = carry
        qc, kc, vc, lic, lfc = xs
        b = jnp.cumsum(lfc, axis=-1)
        Dm = jnp.where(causal, b[..., :, None] - b[..., None, :] + lic[..., None, :], NEG)
        inter = b + m[..., None]
        m_t = jnp.maximum(inter, jnp.max(Dm, axis=-1))
        w_intra = jnp.exp(Dm - m_t[..., None])
        w_inter = jnp.exp(inter - m_t)
        A = w_intra * jnp.einsum('bhte,bhse->bhts', qc, kc)
        num = (w_inter[..., None] * jnp.einsum('bhte,bhev->bhtv', qc, C)
               + jnp.einsum('bhts,bhsv->bhtv', A, vc))
        den = w_inter * jnp.einsum('bhte,bhe->bht', qc, n) + jnp.sum(A, axis=-1)
        h = num / jnp.maximum(jnp.abs(den), jnp.exp(-m_t))[..., None]
        g = b[..., -1:] - b + lic
        m_new = jnp.maximum(b[..., -1] + m, jnp.max(g, axis=-1))
        decay = jnp.exp(b[..., -1] + m - m_new)
        wk = jnp.exp(g - m_new[..., None])
        C = decay[..., None, None] * C + jnp.einsum('bhs,bhse,bhsv->bhev', wk, kc, vc)
        n = decay[..., None] * n + jnp.einsum('bhs,bhse->bhe', wk, kc)
        return (C, n, m_new), h

    init = (jnp.zeros((B, H, K, V), jnp.float32),
            jnp.zeros((B, H, K), jnp.float32),
            jnp.full((B, H), NEG, jnp.float32))
    xs = (vec_chunks(q), vec_chunks(k), vec_chunks(v), gate_chunks(log_i), gate_chunks(log_f))
    _, hs = lax.scan(step, init, xs)
    return hs.transpose(1, 0, 3, 2, 4).reshape(B, S, H, V)


def softcap(t):
    return GATE_SOFTCAP * jnp.tanh(t / GATE_SOFTCAP)


def setup_inputs(seed: int = 0) -> dict:
    key = jax.random.key(seed)
    ks = jax.random.split(key, 16)
    nrm = jax.random.normal
    f32 = jnp.float32
    return {
        'x': nrm(ks[0], (BATCH, SEQ, D_MODEL), f32),
        'norm_mix_g': 1.0 + 0.02 * nrm(ks[1], (DEPTH, D_MODEL), f32),
        'w_in': nrm(ks[2], (DEPTH, D_MODEL, IN_WIDTH), f32) * D_MODEL ** -0.5,
        'b_igate': 0.1 * nrm(ks[3], (DEPTH, MLSTM_HEADS), f32),
        'b_fgate': jnp.linspace(3.0, 6.0, MLSTM_HEADS, dtype=f32)[None] + 0.1 * nrm(ks[4], (DEPTH, MLSTM_HEADS), f32),
        'mlstm_norm_g': 1.0 + 0.02 * nrm(ks[5], (DEPTH, MLSTM_V_WIDTH), f32),
        'w_attn_branch': nrm(ks[6], (DEPTH, ATTN_OUT_WIDTH, D_MODEL), f32) * ATTN_OUT_WIDTH ** -0.5,
        'w_mlstm_branch': nrm(ks[7], (DEPTH, MLSTM_V_WIDTH, D_MODEL), f32) * MLSTM_V_WIDTH ** -0.5,
        'w_gate': nrm(ks[8], (DEPTH, D_MODEL, 2 * D_MODEL), f32) * D_MODEL ** -0.5,
        'b_gate': 0.02 * nrm(ks[9], (DEPTH, 2 * D_MODEL), f32),
        'w_out': nrm(ks[10], (DEPTH, D_MODEL, D_MODEL), f32) * D_MODEL ** -0.5,
        'norm_mlp_g': 1.0 + 0.02 * nrm(ks[11], (DEPTH, D_MODEL), f32),
        'w_up': nrm(ks[12], (DEPTH, D_MODEL, D_FF), f32) * D_MODEL ** -0.5,
        'w_down': nrm(ks[13], (DEPTH, D_FF, D_MODEL), f32) * D_FF ** -0.5,
        'norm_final_g': 1.0 + 0.02 * nrm(ks[14], (D_MODEL,), f32),
    }


def reference(x, norm_mix_g, w_in, b_igate, b_fgate, mlstm_norm_g, w_attn_branch,
              w_mlstm_branch, w_gate, b_gate, w_out, norm_mlp_g, w_up, w_down, norm_final_g):
    B, S, _ = x.shape
    slopes = jnp.asarray(attn_alibi_slopes())
    split_at = [int(c) for c in np.cumsum(IN_SPLITS)[:-1]]
    h = x
    for layer in range(DEPTH):
        xn = rms_norm(h, norm_mix_g[layer])
        proj = xn @ w_in[layer]
        a_q, a_k, a_v, m_q, m_k, m_v, m_o, m_i, m_f = jnp.split(proj, split_at, axis=-1)

        gshape = (B, S, N_GROUPS, ATTN_HEADS_PER_GROUP, ATTN_HEAD_DIM)
        a_q, a_k, a_v = a_q.reshape(gshape), a_k.reshape(gshape), a_v.reshape(gshape)
        outs, lses = [], []
        for g, (window, dilation) in enumerate(ATTN_GROUPS):
            o_g, lse_g = dilated_window_attention(a_q[:, :, g], a_k[:, :, g], a_v[:, :, g],
                                                  window, dilation, slopes[g])
            outs.append(o_g)
            lses.append(lse_g)
        outs = jnp.stack(outs, axis=2)
        mix_w = jax.nn.softmax(jnp.stack(lses, axis=2), axis=2)
        attn = jnp.sum(mix_w[..., None] * outs, axis=2).reshape(B, S, ATTN_OUT_WIDTH).astype(x.dtype)

        q = m_q.reshape(B, S, MLSTM_HEADS, MLSTM_QK_DIM)
        k = m_k.reshape(B, S, MLSTM_HEADS, MLSTM_QK_DIM) * (MLSTM_QK_DIM ** -0.5)
        v = m_v.reshape(B, S, MLSTM_HEADS, MLSTM_V_DIM)
        log_i = softcap((m_i + b_igate[layer]).astype(jnp.float32))
        log_f = jax.nn.log_sigmoid(softcap((m_f + b_fgate[layer]).astype(jnp.float32)))
        ht = mlstm_chunkwise(q, k, v, log_i, log_f)
        ht = ht * lax.rsqrt(jnp.mean(ht * ht, axis=-1, keepdims=True) + RMS_EPS)
        ht = ht * mlstm_norm_g[layer].astype(jnp.float32).reshape(MLSTM_HEADS, MLSTM_V_DIM)
        mlstm = (ht.reshape(B, S, MLSTM_V_WIDTH) * jax.nn.sigmoid(m_o.astype(jnp.float32))).astype(x.dtype)

        br_a = attn @ w_attn_branch[layer]
        br_m = mlstm @ w_mlstm_branch[layer]
        gates = jax.nn.sigmoid(xn @ w_gate[layer] + b_gate[layer])
        g_a, g_m = jnp.split(gates, 2, axis=-1)
        h = h + (g_a * br_a + g_m * br_m) @ w_out[layer]

        hn = rms_norm(h, norm_mlp_g[layer])
        u = jnp.square(jax.nn.relu(hn @ w_up[layer]))
        h = h + u @ w_down[layer]
    return rms_norm(h, norm_final_g)
```

```python
import math
from contextlib import ExitStack
import numpy as np
import concourse.bass as bass
import concourse.mybir as mybir
from concourse.bass_utils import run_bass_kernel_spmd

F32, BF16 = mybir.dt.float32, mybir.dt.bfloat16
AF = mybir.ActivationFunctionType
ALU = mybir.AluOpType

D = 4096
SEQ = 8192
NOWN = 2048
NEXT = 8192
OWN0 = NEXT - NOWN
HALO0 = OWN0 - 2048
IN_W = 21520
C_AQ, C_AK, C_AV, C_MQ, C_MK, C_MV, C_MO, C_MI = 0, 3072, 6144, 9216, 11264, 13312, 17408, 21504
DFF = 16384
EPS = 1e-6
GDIL = (1, 4, 16)


class Eng:
    def __init__(self, K, h, name):
        self.h = h
        self.sem = K.new_sem(name)
        self.cnt = 0
        self.seen = {}
        self.name = name

    def wait(self, ev):
        sem, val = ev
        k = id(sem)
        if sem is self.sem and self.name == "pe":
            return
        if self.seen.get(k, 0) >= val:
            return
        self.h.wait_ge(sem, val)
        self.seen[k] = val


class Buf:
    __slots__ = ("w", "r", "ds", "excl")

    def __init__(self):
        self.w = {}
        self.r = {}
        self.ds = None
        self.excl = False


class T:
    __slots__ = ("ap", "buf")

    def __init__(self, ap, buf=None):
        self.ap = ap
        self.buf = buf if buf is not None else Buf()

    def __getitem__(self, idx):
        return self.ap[idx]


class Rot:
    def __init__(self, items):
        self.items = items
        self.i = 0

    def next(self):
        t = self.items[self.i % len(self.items)]
        self.i += 1
        return t


class Kern:
    def __init__(self, nc):
        self.nc = nc
        self.gs = ExitStack()
        self.nsem = 0
        self.uid = 0
        self.PE = Eng(self, nc.tensor, "pe")
        self.ACT = Eng(self, nc.scalar, "act")
        self.DVE = Eng(self, nc.vector, "dve")
        self.POOL = Eng(self, nc.gpsimd, "pool")
        self.SP = Eng(self, nc.sync, "sp")
        self.engs = [self.PE, self.ACT, self.DVE, self.POOL, self.SP]
        self.dpool = []
        self.dlive = []
        self.phase_bufs = []

    def new_sem(self, name):
        self.nsem += 1
        return self.gs.enter_context(self.nc.semaphore(f"{name}_{self.nsem}"))

    def name(self, p):
        self.uid += 1
        return f"{p}{self.uid}"

    def _pre(self, eng, reads, writes):
        for t in reads:
            for ev in t.buf.w.values():
                eng.wait(ev)
            if t.buf.excl:
                for ev in t.buf.r.values():
                    if ev[0] is not eng.sem:
                        eng.wait(ev)
        for t in writes:
            for ev in t.buf.w.values():
                eng.wait(ev)
            for ev in t.buf.r.values():
                eng.wait(ev)

    def _post(self, ev, reads, writes, part=False):
        k = id(ev[0])
        for t in reads:
            t.buf.r[k] = ev
        for t in writes:
            if part:
                t.buf.w[k] = ev
            else:
                t.buf.w = {k: ev}
                t.buf.r = {}

    def op(self, eng, fn, reads=(), writes=(), part=False):
        self._pre(eng, reads, writes)
        ins = fn()
        eng.cnt += 1
        ins.then_inc(eng.sem, 1)
        self._post((eng.sem, eng.cnt), reads, writes, part)

    def dma(self, eng, out, in_, sb, reads=(), writes=(), part=False, **kw):
        b = sb.buf
        if b.ds is None:
            if self.dpool:
                b.ds = self.dpool.pop()
            else:
                b.ds = [self.new_sem("d"), 0]
                self.dlive.append(b.ds)
            self.phase_bufs.append(b)
        self._pre(eng, reads, writes)
        ins = eng.h.dma_start(out=out, in_=in_, **kw)
        b.ds[1] += 16
        ins.then_inc(b.ds[0], 16)
        self._post((b.ds[0], b.ds[1]), reads, writes, part)

    def barrier(self, engs=None):
        evs = [(e.sem, e.cnt) for e in self.engs if e.cnt > 0]
        evs += [(d[0], d[1]) for d in self.dlive if d[1] > 0]
        for e in (engs or self.engs):
            for ev in evs:
                e.wait(ev)

    def end_phase(self):
        self.barrier()
        for b in self.phase_bufs:
            self.dpool.append(b.ds)
            b.ds = None
        self.phase_bufs = []


class Phase:
    def __init__(self, K):
        self.K = K
        self.st = ExitStack()

    def __enter__(self):
        self.st.__enter__()
        return self

    def __exit__(self, *a):
        self.K.end_phase()
        return self.st.__exit__(*a)

    def sb(self, shape, dt, nm="t"):
        return T(self.st.enter_context(self.K.nc.sbuf_tensor(self.K.name(nm), list(shape), dt)))

    def ps(self, shape, dt, nm="p"):
        t = T(self.st.enter_context(self.K.nc.psum_tensor(self.K.name(nm), list(shape), dt)))
        t.buf.excl = True
        return t

    def sbs(self, n, shape, dt, nm="t"):
        return Rot([self.sb(shape, dt, nm) for _ in range(n)])

    def pss(self, n, shape, dt, nm="p"):
        return Rot([self.ps(shape, dt, nm) for _ in range(n)])


def _alibi_slope_list(n):
    def pow2(m):
        start = 2.0 ** (-(2.0 ** -(math.log2(m) - 3)))
        return [start ** (i + 1) for i in range(m)]
    if math.log2(n).is_integer():
        return pow2(n)
    c = 2 ** math.floor(math.log2(n))
    return pow2(c) + _alibi_slope_list(2 * c)[0::2][: n - c]


def _consts():
    s = np.asarray(sorted(_alibi_slope_list(24), reverse=True), dtype=np.float32).reshape(3, 8)
    cst = np.zeros((128, 384), np.float32)
    cst[:, 0:128] = np.eye(128, dtype=np.float32)
    i = np.arange(128)
    cst[:, 128:256] = (i[:, None] <= i[None, :]).astype(np.float32)
    cst[:, 256:384] = 1.0
    eb = np.full((128, 24, 256), -30000.0, np.float32)
    k = i[:, None].astype(np.float32)
    q = i[None, :].astype(np.float32)
    for g in range(3):
        for j in range(8):
            relp = q - k + 128.0
            rels = q - k
            bp = np.where(relp <= 128.0, -s[g, j] * relp * GDIL[g], -30000.0)
            bs = np.where(rels >= 0.0, -s[g, j] * rels * GDIL[g], -30000.0)
            eb[:, g * 8 + j, 0:128] = bp
            eb[:, g * 8 + j, 128:256] = bs
    return cst, eb.reshape(128, 24 * 256)


def build_program(dbg=(), stop_after=99):
    nc = bass.Bass("TRN2", target_bir_lowering=False)
    K = Kern(nc)
    PE, ACT, DVE, POOL, SP = K.PE, K.ACT, K.DVE, K.POOL, K.SP

    def din(name, shape, dt=F32):
        return nc.dram_tensor(name, list(shape), dt, kind="ExternalInput").ap()

    def dscr(name, shape, dt):
        kind = "ExternalOutput" if name in dbg else "Internal"
        return nc.dram_tensor(name, list(shape), dt, kind=kind).ap()

    x_ext = din("x_ext", [NEXT, D])
    w_in = din("w_in", [D, IN_W])
    w_gate = din("w_gate", [D, 2 * D])
    w_ab = din("w_ab", [1024, D])
    w_mb = din("w_mb", [D, D])
    w_out = din("w_out", [D, D])
    w_up = din("w_up", [D, DFF])
    w_down = din("w_down", [DFF, D])
    g_mix = din("g_mix", [D])
    g_mlp = din("g_mlp", [D])
    g_fin = din("g_fin", [D])
    g_h = din("g_h", [D])
    b_g = din("b_g", [128, 64])
    b_if = din("b_if", [16])
    cst_d = din("cst", [128, 384])
    eb_d = din("ebias", [128, 24 * 256])
    hv_d = din("halo_valid", [128, 1])
    out_d = nc.dram_tensor("out", [NOWN, D], F32, kind="ExternalOutput").ap()

    XT = dscr("XT", [D, NEXT], BF16)
    AQT = dscr("AQT", [3, 1024, NOWN], BF16)
    AKT = dscr("AKT", [3, 1024, NOWN], BF16)
    AKH = dscr("AKH", [3, 1024, 2048], BF16)
    AV = dscr("AV", [3, NOWN, 1024], BF16)
    AVH = dscr("AVH", [3, 2048, 1024], BF16)
    MQT = dscr("MQT", [2048, NOWN], BF16)
    MKT = dscr("MKT", [2048, NOWN], BF16)
    MK = dscr("MK", [NEXT, 2048], BF16)
    MV = dscr("MV", [NEXT, 4096], BF16)
    MO = dscr("MO", [NOWN, 4096], BF16)
    MIF = dscr("MIF", [NEXT, 16], F32)
    GT = dscr("GT", [2 * D, NOWN], BF16)
    NUM = dscr("NUM", [3, NOWN, 8, 132], F32)
    ATT = dscr("ATT", [1024, NOWN], BF16)
    MT = dscr("MT", [D, NOWN], BF16)
    H1 = dscr("H1", [NOWN, D], F32)
    HT = dscr("HT", [D, NOWN], BF16)
    UT = dscr("UT", [DFF, NOWN], BF16)
    H2 = dscr("H2", [NOWN, D], F32)

    cs = K.gs
    def gsb(shape, dt, nm):
        return T(cs.enter_context(nc.sbuf_tensor(nm, list(shape), dt)))
    cst = gsb([128, 384], F32, "cst_sb")
    identb = gsb([128, 128], BF16, "identb")
    K.dma(SP, cst.ap[:], cst_d[:, :], cst, writes=[cst])
    K.op(DVE, lambda: nc.vector.tensor_copy(out=identb.ap[:], in_=cst.ap[:, 0:128]), reads=[cst], writes=[identb])
    U_f = cst.ap[:, 128:256]
    ones_f = cst.ap[:, 256:384]

    def norm_pass(src, ntok, gvec, dstT=None, dst=None, dst_tm=None, ntm=0):
        with Phase(K) as ph:
            gbc = ph.sb([128, D], F32, "gbc")
            K.dma(SP, gbc.ap[:], gvec.partition_broadcast(128), gbc, writes=[gbc])
            xts = ph.sbs(5, [128, D], F32, "xt")
            junk = ph.sb([128, D], BF16, "junk")
            ssqs = ph.sbs(4, [128, 1], F32, "ssq")
            rstds = ph.sbs(4, [128, 1], F32, "rstd")
            if dstT is not None:
                xns = ph.sbs(3, [128, D], BF16, "xn")
                stgs = ph.sbs(2, [128, 32, 512], BF16, "stg")
                pts = ph.pss(6, [128, 512], BF16, "pt")
            else:
                ys = ph.sbs(3, [128, D], F32, "y")
            nt = ntok // 128
            loaded = {}
            st_ = {}

            def load(i):
                xt = xts.next()
                K.dma(SP, xt.ap[:], src[i * 128:(i + 1) * 128, :], xt, writes=[xt])
                loaded[i] = xt

            def S0(i):
                if i + 2 < nt:
                    load(i + 2)
                xt = loaded.pop(i)
                ssq = ssqs.next()
                rstd = rstds.next()
                st_[i] = dict(xt=xt, ssq=ssq, rstd=rstd)
                K.op(ACT, lambda: nc.scalar.activation(out=junk.ap[:], in_=xt.ap[:], func=AF.Square, accum_out=ssq.ap[:]),
                     reads=[xt], writes=[junk, ssq])
                K.op(DVE, lambda: nc.vector.tensor_scalar(out=rstd.ap[:], in0=ssq.ap[:], scalar1=1.0 / D, scalar2=EPS,
                                                          op0=ALU.mult, op1=ALU.add), reads=[ssq], writes=[rstd])

            def S1(i):
                rstd = st_[i]["rstd"]
                K.op(ACT, lambda: nc.scalar.activation(out=rstd.ap[:], in_=rstd.ap[:], func=AF.Sqrt), reads=[rstd], writes=[rstd])

            def S2(i):
                d = st_[i]
                xt, rstd = d["xt"], d["rstd"]
                K.op(DVE, lambda: nc.vector.reciprocal(out=rstd.ap[:], in_=rstd.ap[:]), reads=[rstd], writes=[rstd])
                if dstT is None:
                    y = ys.next()
                    K.op(DVE, lambda: nc.vector.scalar_tensor_tensor(out=y.ap[:], in0=xt.ap[:], scalar=rstd.ap[:, 0:1], in1=gbc.ap[:],
                                                                     op0=ALU.mult, op1=ALU.mult), reads=[xt, rstd, gbc], writes=[y])
                    K.dma(SP, dst[i * 128:(i + 1) * 128, :], y.ap[:], y, reads=[y])
                    return
                xn = xns.next()
                d["xn"] = xn
                K.op(DVE, lambda: nc.vector.scalar_tensor_tensor(out=xn.ap[:], in0=xt.ap[:], scalar=rstd.ap[:, 0:1], in1=gbc.ap[:],
                                                                 op0=ALU.mult, op1=ALU.mult), reads=[xt, rstd, gbc], writes=[xn])
                if dst_tm is not None and i < ntm:
                    K.dma(SP, dst_tm[i * 128:(i + 1) * 128, :], xn.ap[:], xn, reads=[xn])

            cur = {"stg": None, "ev": 0}

            def S3(i):
                if dstT is None:
                    st_.pop(i)
                    return
                xn = st_.pop(i)["xn"]
                if i % 4 == 0:
                    cur["stg"] = stgs.next()
                stg = cur["stg"]
                sub = i % 4
                for q4 in range(8):
                    pt = pts.next()

                    def tr():
                        ins = None
                        for jj in range(4):
                            kc = q4 * 4 + jj
                            ins = nc.tensor.transpose(out=pt.ap[:, jj * 128:(jj + 1) * 128], in_=xn.ap[:, kc * 128:(kc + 1) * 128],
                                                      identity=identb.ap[:])
                        return ins
                    K.op(PE, tr, reads=[xn, identb], writes=[pt])
                    dsto = stg.ap[:, q4 * 4:(q4 + 1) * 4, sub * 128:(sub + 1) * 128]
                    srci = pt.ap[:, :].rearrange("p (a b) -> p a b", a=4)
                    if cur["ev"] % 2 == 0:
                        K.op(ACT, lambda: nc.scalar.copy(out=dsto, in_=srci), reads=[pt], writes=[stg], part=True)
                    else:
                        K.op(DVE, lambda: nc.vector.tensor_copy(out=dsto, in_=srci), reads=[pt], writes=[stg], part=True)
                    cur["ev"] += 1
                if sub == 3:
                    t0 = (i - 3) * 128
                    dv = dstT.rearrange("(kc p) t -> p kc t", p=128)
                    for h in range(2):
                        K.dma(SP, dv[:, h * 16:(h + 1) * 16, t0:t0 + 512], stg.ap[:, h * 16:(h + 1) * 16, :], stg, reads=[stg])

            load(0)
            if nt > 1:
                load(1)
            for step in range(nt + 3):
                for lag, fn in enumerate((S0, S1, S2, S3)):
                    i = step - lag
                    if 0 <= i < nt:
                        fn(i)

    class Gemm:
        def __init__(self, ph, KC, Tn, npan=2):
            self.ph, self.KC, self.Tn = ph, KC, Tn
            self.wps = ph.sbs(npan, [128, KC, 512], BF16, "wp")
            self.pss = ph.pss(4, [128, 512], F32, "gps")
            self.tick = None

        def load_panel(self, w, c0, ncols, KC=None):
            KC = KC or self.KC
            wp = self.wps.next()
            wv = w.rearrange("(kc p) n -> p kc n", p=128)
            nsp = 2 if KC >= 16 else 1
            if ncols < 64:
                nsp = 8
            for h in range(nsp):
                k0, k1 = h * KC // nsp, (h + 1) * KC // nsp
                K.dma(POOL, wp.ap[:, k0:k1, 0:ncols], wv[:, k0:k1, c0:c0 + ncols], wp, writes=[wp], part=(h > 0))
            return wp

        def run(self, panels):
            def ld(q):
                if "pre" in q:
                    q["pre"]()
                return self.load_panel(q["w"], q["c0"], q["ncols"], q.get("KC"))
            nxt = ld(panels[0])
            for i, p in enumerate(panels):
                cur = nxt
                if i + 1 < len(panels):
                    nxt = ld(panels[i + 1])
                p["jobs"](cur)

        def group(self, mms, reads):
            ps = self.pss.next()

            def f():
                ins = None
                n = len(mms)
                for i, (o, l, r) in enumerate(mms):
                    ins = nc.tensor.matmul(o(ps), lhsT=l, rhs=r, start=(i == 0), stop=(i == n - 1))
                return ins
            K.op(PE, f, reads=reads, writes=[ps])
            if self.tick is not None:
                self.tick()
            return ps

    evq = [0]

    def evac(out_ap, in_ap, reads, writes, part=False, scale=None, func=None, bias=None, eng=None):
        use_act = (func is not None) or (eng == "act") or (eng is None and evq[0] % 2 == 0)
        evq[0] += 1
        if use_act:
            kw = {}
            if scale is not None:
                kw["scale"] = scale
            if bias is not None:
                kw["bias"] = bias
            K.op(ACT, lambda: nc.scalar.activation(out=out_ap, in_=in_ap, func=(func or AF.Copy), **kw),
                 reads=reads, writes=writes, part=part)
        else:
            if scale is not None:
                K.op(DVE, lambda: nc.vector.tensor_scalar(out=out_ap, in0=in_ap, scalar1=scale, scalar2=None, op0=ALU.mult),
                     reads=reads, writes=writes, part=part)
            else:
                K.op(DVE, lambda: nc.vector.tensor_copy(out=out_ap, in_=in_ap), reads=reads, writes=writes, part=part)

    def proj_alloc(ph, nat=1):
        R = dict(G=Gemm(ph, 32, 1024), ATs=ph.sbs(nat, [128, 32, 1024], BF16, "AT"), ATnext=None,
                 stg_tm=ph.sbs(4, [128, 512], BF16, "stm"), stg_fm=ph.sbs(2, [128, 1024], BF16, "sfm"),
                 bif=ph.sb([128, 16], F32, "bif"), bgs=ph.sb([128, 64], F32, "bgs"),
                 gsm=ph.sbs(2, [128, 16], F32, "gsm"), gsm2=ph.sbs(2, [128, 16], F32, "gsm2"))
        K.dma(SP, R["bif"].ap[:], b_if.partition_broadcast(128), R["bif"], writes=[R["bif"]])
        K.dma(SP, R["bgs"].ap[:], b_g[:, :], R["bgs"], writes=[R["bgs"]])
        return R

    def load_AT(R, ti):
        AT = R["ATs"].next()
        e0 = ti * 1024
        xv = XT.rearrange("(kc p) t -> p kc t", p=128)
        for h in range(4):
            K.dma(SP, AT.ap[:, h * 8:(h + 1) * 8, :], xv[:, h * 8:(h + 1) * 8, e0:e0 + 1024], AT, writes=[AT], part=(h > 0))
        return AT

    def proj_phase(tis, bgf=None, nat=1):
        with Phase(K) as ph:
            R = proj_alloc(ph, nat)
            bg = bgf(ph) if bgf is not None else None
            if bg is not None:
                cnt = [0]

                def tick():
                    cnt[0] += 1
                    if cnt[0] % 2 == 0:
                        next(bg, None)
                R["G"].tick = tick
            R["ATnext"] = load_AT(R, tis[0])
            for i, ti in enumerate(tis):
                R["AT"] = R["ATnext"]
                if nat > 1 and i + 1 < len(tis):
                    R["ATnext"] = load_AT(R, tis[i + 1])
                proj_tile(ti, R)
                if nat == 1 and i + 1 < len(tis):
                    R["ATnext"] = load_AT(R, tis[i + 1])
            if bg is not None:
                R["G"].tick = None
                for _ in bg:
                    pass

    def proj_tile(ti, R):
        e0 = ti * 1024
        own = e0 >= OWN0
        halo = (not own) and e0 >= HALO0
        o0 = e0 - OWN0
        u0 = e0 - HALO0
        if True:
            G, AT, stg_tm, stg_fm, bif, bgs, gsm, gsm2 = (R[k] for k in ("G", "AT", "stg_tm", "stg_fm", "bif", "bgs", "gsm", "gsm2"))
            panels = []

            def nat_cols(ts):
                return lambda kc: AT.ap[:, kc, ts * 128:(ts + 1) * 128]

            def tm_panel(c0, subtiles, dst_fn, scale=None, func=None):
                def jobs(wp):
                    for st_i, (lf, M) in enumerate(subtiles):
                        M = M or 128
                        ps = G.group([((lambda p, M=M: p.ap[0:M, :]), lf(kc), wp.ap[:, kc, :]) for kc in range(32)], [AT, wp])
                        st = stg_tm.next()
                        evac(st.ap[0:M, :], ps.ap[0:M, :], [ps], [st], scale=scale, func=func)
                        for (p0, p1, dap) in dst_fn(st_i):
                            K.dma(SP, dap, st.ap[p0:p1, :], st, reads=[st])
                panels.append(dict(w=w_in, c0=c0, ncols=512, jobs=jobs))

            def fm_panel(w, c0, perm, dst_fn, scale=None, func=None, bias_fn=None):
                def jobs(wp):
                    for cb in range(4):
                        st = stg_fm.next()
                        for tg in range(2):
                            ps = G.group([(lambda p: p.ap[:, :], wp.ap[:, kc, cb * 128:(cb + 1) * 128],
                                           AT.ap[:, kc, tg * 512:(tg + 1) * 512]) for kc in range(32)], [AT, wp])
                            if perm == 1:
                                o = st.ap[:, :]
                                o = o[:, tg * 512:(tg + 1) * 512]
                                i_ = ps.ap[:, :]
                            else:
                                d = perm
                                per = 1024 // d
                                o = st.ap[:, :].rearrange("p (r l) -> p r l", r=d)[:, :, tg * (512 // d):(tg + 1) * (512 // d)]
                                i_ = ps.ap[:, :].rearrange("p (l r) -> p r l", r=d)
                            b = bias_fn(c0 // 128 + cb) if bias_fn else None
                            evac(o, i_, [ps], [st], part=True, scale=scale, func=func, bias=b)
                        dst_fn(c0 // 128 + cb, st)
                panels.append(dict(w=w, c0=c0, ncols=512, jobs=jobs))

            def add_attn_kv(groups, is_own):
                KD = AKT if is_own else AKH
                VD = AV if is_own else AVH
                toff = o0 if is_own else u0
                for g in groups:
                    d = GDIL[g]
                    for pb in range(2):
                        cbase = C_AK + g * 1024 + pb * 512

                        def kdst(cbg, st, g=g, d=d):
                            row0 = (cbg - (C_AK + g * 1024) // 128) * 128
                            if d == 1:
                                K.dma(SP, KD[g, row0:row0 + 128, toff:toff + 1024], st.ap[:, :], st, reads=[st])
                            else:
                                per = 1024 // d
                                dv = KD[g, row0:row0 + 128, :].rearrange("p (r l) -> p r l", r=d)[:, :, toff // d:toff // d + per]
                                K.dma(SP, dv, st.ap[:, :].rearrange("p (r l) -> p r l", r=d), st, reads=[st])
                        fm_panel(w_in, cbase, d, kdst)
                    for pb in range(2):
                        cbase = C_AV + g * 1024 + pb * 512
                        if d == 1:
                            subt = [(nat_cols(ts), None) for ts in range(8)]

                            def vdst(si, pb=pb, g=g):
                                r0 = toff + si * 128
                                return [(0, 128, VD[g, r0:r0 + 128, pb * 512:(pb + 1) * 512])]
                        elif d == 4:
                            subt = []
                            for r in range(4):
                                for hl in range(2):
                                    s0 = r + 512 * hl
                                    subt.append(((lambda kc, s0=s0: AT.ap[:, kc, s0:s0 + 509:4]), None))

                            def vdst(si, pb=pb, g=g):
                                r, hl = si // 2, si % 2
                                p0 = r * 512 + toff // 4 + hl * 128
                                return [(0, 128, VD[g, p0:p0 + 128, pb * 512:(pb + 1) * 512])]
                        else:
                            subt = []
                            for rho in range(16):
                                subt.append(((lambda kc, rho=rho: AT.ap[:, kc, rho:rho + 16 * 63 + 1:16]), 64))

                            def vdst(si, pb=pb, g=g):
                                p0 = si * 128 + toff // 16
                                return [(0, 64, VD[g, p0:p0 + 64, pb * 512:(pb + 1) * 512])]
                        tm_panel(cbase, subt, vdst)

            nat8 = [(nat_cols(ts), None) for ts in range(8)]
            if own:
                for g in range(3):
                    d = GDIL[g]
                    for pb in range(2):
                        def qdst(cbg, st, g=g, d=d):
                            row0 = (cbg - (g * 1024) // 128) * 128
                            if d == 1:
                                K.dma(SP, AQT[g, row0:row0 + 128, o0:o0 + 1024], st.ap[:, :], st, reads=[st])
                            else:
                                per = 1024 // d
                                dv = AQT[g, row0:row0 + 128, :].rearrange("p (r l) -> p r l", r=d)[:, :, o0 // d:o0 // d + per]
                                K.dma(SP, dv, st.ap[:, :].rearrange("p (r l) -> p r l", r=d), st, reads=[st])
                        fm_panel(w_in, C_AQ + g * 1024 + pb * 512, d, qdst)
                add_attn_kv([0, 1, 2], True)
                for sec, DT, sc in ((C_MQ, MQT, None),):
                    for pb in range(4):
                        def mdst(cbg, st, sec=sec, DT=DT):
                            row0 = (cbg - sec // 128) * 128
                            K.dma(SP, DT[row0:row0 + 128, o0:o0 + 1024], st.ap[:, :], st, reads=[st])
                        fm_panel(w_in, sec + pb * 512, 1, mdst, scale=sc)
                for pb in range(16):
                    def gdst(cbg, st):
                        K.dma(SP, GT[cbg * 128:(cbg + 1) * 128, o0:o0 + 1024], st.ap[:, :], st, reads=[st])
                    fm_panel(w_gate, pb * 512, 1, gdst, func=AF.Sigmoid, bias_fn=lambda cbg: bgs.ap[:, cbg:cbg + 1])
                for pb in range(8):
                    tm_panel(C_MO + pb * 512, nat8,
                             (lambda si, pb=pb: [(0, 128, MO[o0 + si * 128:o0 + (si + 1) * 128, pb * 512:(pb + 1) * 512])]),
                             func=AF.Sigmoid)
            elif halo:
                if u0 == 0:
                    add_attn_kv([2], False)
                else:
                    add_attn_kv([0, 1, 2], False)
            for pb in range(4):
                tm_panel(C_MK + pb * 512, nat8,
                         (lambda si, pb=pb: [(0, 128, MK[e0 + si * 128:e0 + (si + 1) * 128, pb * 512:(pb + 1) * 512])]),
                         scale=1.0 / 16.0)
            for pb in (range(8) if own else ()):
                tm_panel(C_MV + pb * 512, nat8,
                         (lambda si, pb=pb: [(0, 128, MV[e0 + si * 128:e0 + (si + 1) * 128, pb * 512:(pb + 1) * 512])]))

            def gate_jobs(wp):
                for ts in range(8):
                    ps = G.group([(lambda p: p.ap[:, 0:16], AT.ap[:, kc, ts * 128:(ts + 1) * 128], wp.ap[:, kc, 0:16]) for kc in range(32)],
                                 [AT, wp])
                    z = gsm.next()
                    r = gsm2.next()
                    K.op(DVE, lambda: nc.vector.tensor_tensor(out=z.ap[:], in0=ps.ap[:, 0:16], in1=bif.ap[:], op=ALU.add),
                         reads=[ps, bif], writes=[z])
                    K.op(ACT, lambda: nc.scalar.activation(out=z.ap[:], in_=z.ap[:], func=AF.Tanh, scale=1.0 / 15.0), reads=[z], writes=[z])
                    K.op(DVE, lambda: nc.vector.tensor_scalar(out=r.ap[:, 0:8], in0=z.ap[:, 0:8], scalar1=15.0, scalar2=None, op0=ALU.mult),
                         reads=[z], writes=[r], part=True)
                    K.op(ACT, lambda: nc.scalar.activation(out=z.ap[:, 8:16], in_=z.ap[:, 8:16], func=AF.Exp, scale=-15.0), reads=[z], writes=[z])
                    K.op(DVE, lambda: nc.vector.tensor_scalar(out=z.ap[:, 8:16], in0=z.ap[:, 8:16], scalar1=1.0, scalar2=None, op0=ALU.add),
                         reads=[z], writes=[z])
                    K.op(ACT, lambda: nc.scalar.activation(out=z.ap[:, 8:16], in_=z.ap[:, 8:16], func=AF.Ln), reads=[z], writes=[z])
                    K.op(DVE, lambda: nc.vector.tensor_scalar(out=r.ap[:, 8:16], in0=z.ap[:, 8:16], scalar1=-1.0, scalar2=None, op0=ALU.mult),
                         reads=[z], writes=[r], part=True)
                    K.dma(SP, MIF[e0 + ts * 128:e0 + (ts + 1) * 128, :], r.ap[:], r, reads=[r])
            panels.append(dict(w=w_in, c0=C_MI, ncols=16, jobs=gate_jobs))
            G.run(panels)

    def attention():
        with Phase(K) as ph:
            ebf = ph.sb([128, 24 * 256], F32, "ebf")
            K.dma(SP, ebf.ap[:], eb_d[:, :], ebf, writes=[ebf])
            hv = ph.sb([128, 1], F32, "hv")
            K.dma(SP, hv.ap[:], hv_d[:, :], hv, writes=[hv])
            EB = ph.sb([128, 24, 256], BF16, "EB")
            EBH = ph.sb([128, 24, 256], BF16, "EBH")
            K.op(ACT, lambda: nc.scalar.activation(out=ebf.ap[:], in_=ebf.ap[:], func=AF.Exp), reads=[ebf], writes=[ebf])
            K.op(DVE, lambda: nc.vector.tensor_copy(out=EB.ap[:].rearrange("p a b -> p (a b)"), in_=ebf.ap[:]), reads=[ebf], writes=[EB])
            K.op(DVE, lambda: nc.vector.tensor_copy(out=EBH.ap[:].rearrange("p a b -> p (a b)"), in_=ebf.ap[:]), reads=[ebf], writes=[EBH])
            K.op(DVE, lambda: nc.vector.tensor_scalar(out=EBH.ap[:, :, 0:128], in0=EBH.ap[:, :, 0:128], scalar1=hv.ap[:, 0:1], scalar2=None,
                                                      op0=ALU.mult), reads=[EBH, hv], writes=[EBH])
            QTs = ph.sbs(3, [128, 2048], BF16, "QT")
            KTs = ph.sbs(3, [128, 2048], BF16, "KT")
            KHs = ph.sbs(3, [128, 2048], BF16, "KH")
            Vs = ph.sbs(3, [128, 16, 132], BF16, "V")
            VHs = ph.sbs(3, [128, 16, 132], BF16, "VH")
            for r_ in (Vs, VHs):
                for t in r_.items:
                    K.op(DVE, lambda t=t: nc.vector.memset(t.ap[:, :, 128:132], 1.0), writes=[t])
            PTs = ph.sbs(4, [128, 256], BF16, "PT")
            PEs = ph.sbs(5, [128, 256], BF16, "PE")
            Os = ph.sbs(4, [128, 132], F32, "O")
            sps = ph.pss(4, [128, 256], F32, "sps")
            ops_ = ph.pss(4, [128, 132], F32, "ops")
            def stageP(g, j, n, tl):
                QT, KT, KH, V, VH = tl
                if g == 0:
                    hal = (n == 0)
                    pblk = 15 if hal else n - 1
                elif g == 1:
                    hal = (n % 4 == 0)
                    pblk = n + 3 if hal else n - 1
                else:
                    hal = True
                    pblk = n
                Kp, Vp = (KH, VH) if hal else (KT, V)
                sp = sps.next()

                def sc():
                    nc.tensor.matmul(sp.ap[:, 0:128], lhsT=Kp.ap[:, pblk * 128:(pblk + 1) * 128], rhs=QT.ap[:, n * 128:(n + 1) * 128],
                                     start=True, stop=True)
                    return nc.tensor.matmul(sp.ap[:, 128:256], lhsT=KT.ap[:, n * 128:(n + 1) * 128], rhs=QT.ap[:, n * 128:(n + 1) * 128],
                                            start=True, stop=True)
                K.op(PE, sc, reads=[Kp, KT, QT], writes=[sp])
                pt = PTs.next()
                pe_ = PEs.next()
                K.op(ACT, lambda: nc.scalar.activation(out=pt.ap[:], in_=sp.ap[:], func=AF.Exp, scale=1.0 / math.sqrt(128.0)),
                     reads=[sp], writes=[pt])
                tab = EBH if hal else EB
                K.op(DVE, lambda: nc.vector.tensor_tensor(out=pe_.ap[:], in0=pt.ap[:], in1=tab.ap[:, g * 8 + j, :], op=ALU.mult),
                     reads=[pt, tab], writes=[pe_])
                return (g, j, n, pe_, Vp, V, pblk)

            def stageQ(item):
                g, j, n, pe_, Vp, V, pblk = item
                op_ = ops_.next()

                def pv():
                    nc.tensor.matmul(op_.ap[:, 0:129], lhsT=pe_.ap[:, 0:128], rhs=Vp.ap[:, pblk, 0:129], start=True, stop=False)
                    return nc.tensor.matmul(op_.ap[:, 0:129], lhsT=pe_.ap[:, 128:256], rhs=V.ap[:, n, 0:129], start=False, stop=True)
                K.op(PE, pv, reads=[pe_, Vp, V], writes=[op_])
                o = Os.next()
                evac(o.ap[:, 0:129], op_.ap[:, 0:129], [op_], [o])
                if g == 0:
                    t0, stp = n * 128, 1
                elif g == 1:
                    r, m = n // 4, n % 4
                    t0, stp = 4 * (128 * m) + r, 4
                else:
                    t0, stp = n, 16
                K.dma(SP, NUM[g, t0:t0 + 127 * stp + 1:stp, j, 0:129], o.ap[:, 0:129], o, reads=[o])

            pend = []
            for g in range(3):
                for j in range(8):
                    tl = (QTs.next(), KTs.next(), KHs.next(), Vs.next(), VHs.next())
                    QT, KT, KH, V, VH = tl
                    rows = slice(j * 128, (j + 1) * 128)
                    K.dma(SP, QT.ap[:], AQT[g, rows, :], QT, writes=[QT])
                    K.dma(SP, KT.ap[:], AKT[g, rows, :], KT, writes=[KT])
                    K.dma(SP, KH.ap[:], AKH[g, rows, :], KH, writes=[KH])
                    for b4 in range(4):
                        bs = slice(b4 * 4, (b4 + 1) * 4)
                        K.dma(SP, V.ap[:, bs, 0:128], AV[g, :, rows].rearrange("(b p) e -> p b e", p=128)[:, bs, :], V, writes=[V], part=True)
                        K.dma(SP, VH.ap[:, bs, 0:128], AVH[g, :, rows].rearrange("(b p) e -> p b e", p=128)[:, bs, :], VH, writes=[VH], part=True)
                    for n in range(16):
                        pend.append(stageP(g, j, n, tl))
                        if len(pend) > 2:
                            stageQ(pend.pop(0))
            while pend:
                stageQ(pend.pop(0))
        with Phase(K) as ph:
            ns = ph.sbs(2, [128, 3, 8, 132], F32, "ns")
            sm = ph.sbs(2, [128, 8, 132], F32, "sm")
            rc = ph.sbs(2, [128, 8, 1], F32, "rc")
            at = ph.sbs(2, [128, 8, 128], BF16, "at")
            stg = ph.sbs(2, [128, 8, 512], BF16, "astg")
            pts = ph.pss(2, [128, 512], BF16, "apt")
            st = None
            for i in range(16):
                n_ = ns.next()
                for g in range(3):
                    K.dma(SP, n_.ap[:, g, :, :], NUM[g, i * 128:(i + 1) * 128, :, :], n_, writes=[n_], part=(g > 0))
                s_ = sm.next()
                K.op(DVE, lambda: nc.vector.tensor_tensor(out=s_.ap[:], in0=n_.ap[:, 0], in1=n_.ap[:, 1], op=ALU.add), reads=[n_], writes=[s_])
                K.op(DVE, lambda: nc.vector.tensor_tensor(out=s_.ap[:], in0=s_.ap[:], in1=n_.ap[:, 2], op=ALU.add), reads=[n_, s_], writes=[s_])
                r_ = rc.next()
                K.op(DVE, lambda: nc.vector.reciprocal(out=r_.ap[:], in_=s_.ap[:, :, 128:129]), reads=[s_], writes=[r_])
                a_ = at.next()
                K.op(DVE, lambda: nc.vector.tensor_tensor(out=a_.ap[:], in0=s_.ap[:, :, 0:128], in1=r_.ap[:].broadcast_to([128, 8, 128]),
                                                          op=ALU.mult), reads=[s_, r_], writes=[a_])
                if i % 4 == 0:
                    st = stg.next()
                for hh in range(2):
                    pt = pts.next()

                    def tr():
                        ins = None
                        for jj in range(4):
                            ins = nc.tensor.transpose(out=pt.ap[:, jj * 128:(jj + 1) * 128], in_=a_.ap[:, hh * 4 + jj, :], identity=identb.ap[:])
                        return ins
                    K.op(PE, tr, reads=[a_, identb], writes=[pt])
                    evac(st.ap[:, hh * 4:(hh + 1) * 4, (i % 4) * 128:(i % 4 + 1) * 128], pt.ap[:, :].rearrange("p (a b) -> p a b", a=4),
                         [pt], [st], part=True)
                if i % 4 == 3:
                    K.dma(SP, ATT.rearrange("(j p) t -> p j t", p=128)[:, :, (i - 3) * 128:(i + 1) * 128], st.ap[:], st, reads=[st])

    CS_d = dscr("CS", [8, 128, 1024], F32)
    NS_d = dscr("NS", [8, 128, 2], F32)
    Cst = [None] * 8
    nst = [None] * 8
    FIRST_OWN = OWN0 // 128

    def alloc_state(ph, zero):
        for h in range(8):
            Cst[h] = ph.sb([128, 2, 512], F32, "Cst")
            nst[h] = ph.sb([128, 2], F32, "nst")
            if zero:
                K.op(DVE, lambda h=h: nc.vector.memset(Cst[h].ap[:], 0.0), writes=[Cst[h]])
                K.op(DVE, lambda h=h: nc.vector.memset(nst[h].ap[:], 0.0), writes=[nst[h]])
            else:
                K.dma(SP, Cst[h].ap[:].rearrange("p a b -> p (a b)"), CS_d[h, :, :], Cst[h], writes=[Cst[h]])
                K.dma(SP, nst[h].ap[:], NS_d[h, :, :], nst[h], writes=[nst[h]])

    def chunk_gates(gp, if_, w_, wb_, eb_, dc_, tm_):
        def gm():
            nc.tensor.matmul(gp.ap[:, 0:8], lhsT=U_f, rhs=if_.ap[:, 8:16], start=True, stop=True)
            return nc.tensor.matmul(gp.ap[:, 8:16], lhsT=ones_f, rhs=if_.ap[:, 8:16], start=True, stop=True)
        K.op(PE, gm, reads=[cst, if_], writes=[gp])
        K.op(DVE, lambda: nc.vector.tensor_tensor(out=tm_.ap[:], in0=if_.ap[:, 0:8], in1=gp.ap[:, 0:8], op=ALU.subtract),
             reads=[if_, gp], writes=[tm_])
        K.op(ACT, lambda: nc.scalar.activation(out=w_.ap[:], in_=tm_.ap[:], func=AF.Exp), reads=[tm_], writes=[w_])
        if eb_ is not None:
            K.op(ACT, lambda: nc.scalar.activation(out=eb_.ap[:], in_=gp.ap[:, 0:8], func=AF.Exp), reads=[gp], writes=[eb_])
        K.op(ACT, lambda: nc.scalar.activation(out=dc_.ap[:], in_=gp.ap[:, 8:16], func=AF.Exp), reads=[gp], writes=[dc_])
        K.op(DVE, lambda: nc.vector.tensor_copy(out=wb_.ap[:], in_=w_.ap[:]), reads=[w_], writes=[wb_])

    def state_update(h, kt, vs, wb_, dc_, pd, dp2, dd, nt_, Cb=None, nb=None):
        def dm():
            for ec in range(2):
                nc.tensor.matmul(pd.ap[:, ec, :], lhsT=kt.ap[:, h * 256 + ec * 128:h * 256 + (ec + 1) * 128], rhs=vs.ap[:], start=True, stop=True)
            ins = None
            for ec in range(2):
                ins = nc.tensor.matmul(dp2.ap[:, 1 + ec:2 + ec], lhsT=kt.ap[:, h * 256 + ec * 128:h * 256 + (ec + 1) * 128],
                                       rhs=wb_.ap[:, h:h + 1], start=True, stop=True)
            return ins
        K.op(PE, dm, reads=[kt, vs, wb_], writes=[pd, dp2])
        K.op(ACT, lambda: nc.scalar.activation(out=dd.ap[:], in_=pd.ap[:], func=AF.Copy, scale=dc_.ap[:, h:h + 1]),
             reads=[pd, dc_], writes=[dd])
        K.op(DVE, lambda: nc.vector.scalar_tensor_tensor(out=Cst[h].ap[:], in0=Cst[h].ap[:], scalar=dc_.ap[:, h:h + 1], in1=dd.ap[:],
                                                         op0=ALU.mult, op1=ALU.add), reads=[Cst[h], dc_, dd], writes=[Cst[h]])
        if Cb is not None:
            K.op(ACT, lambda: nc.scalar.copy(out=Cb[h].ap[:], in_=Cst[h].ap[:]), reads=[Cst[h]], writes=[Cb[h]])
        K.op(DVE, lambda: nc.vector.tensor_tensor(out=nt_.ap[:], in0=nst[h].ap[:], in1=dp2.ap[:, 1:3], op=ALU.add),
             reads=[nst[h], dp2], writes=[nt_])
        K.op(DVE, lambda: nc.vector.tensor_scalar(out=nst[h].ap[:], in0=nt_.ap[:], scalar1=dc_.ap[:, h:h + 1], scalar2=None, op0=ALU.mult),
             reads=[nt_, dc_], writes=[nst[h]])
        if nb is not None:
            K.op(DVE, lambda: nc.vector.tensor_copy(out=nb[h].ap[:], in_=nst[h].ap[:]), reads=[nst[h]], writes=[nb[h]])

    def mlstm_prefix(ph):
        kts = ph.sbs(2, [128, 2048], BF16, "pktm")
        vts = ph.sbs(2, [128, 4096], BF16, "pvtm")
        ifs = ph.sbs(2, [128, 16], F32, "pifs")
        gw = ph.sbs(2, [128, 8], F32, "pgw")
        gwb = ph.sbs(2, [128, 8], BF16, "pgwb")
        gdc = ph.sbs(2, [128, 8], F32, "pgdc")
        gtmp = ph.sbs(2, [128, 8], F32, "pgtmp")
        vsc = ph.sbs(2, [128, 512], BF16, "pvsc")
        dCd = ph.sbs(1, [128, 2, 512], F32, "pdCd")
        ntmp = ph.sbs(2, [128, 2], F32, "pntmp")
        p_dC = ph.ps([128, 2, 512], F32, "ppdC")
        p_sm = ph.ps([128, 512], F32, "ppsm")
        gate_ps = Rot([T(p_sm.ap[:, i * 16:(i + 1) * 16], p_sm.buf) for i in range(4)])
        den_ps = Rot([T(p_sm.ap[:, 64 + i * 4:64 + (i + 1) * 4], p_sm.buf) for i in range(8)])

        alloc_state(ph, True)

        def gen():
            for c in range(FIRST_OWN):
                e0 = c * 128
                kt, vt, if_ = kts.next(), vts.next(), ifs.next()
                K.dma(SP, kt.ap[:], MK[e0:e0 + 128, :], kt, writes=[kt])
                K.dma(SP, vt.ap[:], MV[e0:e0 + 128, :], vt, writes=[vt])
                K.dma(SP, if_.ap[:], MIF[e0:e0 + 128, :], if_, writes=[if_])
                w_, wb_, dc_, tm_ = gw.next(), gwb.next(), gdc.next(), gtmp.next()
                chunk_gates(gate_ps.next(), if_, w_, wb_, None, dc_, tm_)
                yield
                for h in range(8):
                    vs = vsc.next()
                    K.op(ACT, lambda: nc.scalar.activation(out=vs.ap[:], in_=vt.ap[:, h * 512:(h + 1) * 512], func=AF.Copy, scale=w_.ap[:, h:h + 1]),
                         reads=[vt, w_], writes=[vs])
                    state_update(h, kt, vs, wb_, dc_, p_dC, den_ps.next(), dCd.next(), ntmp.next())
                    yield
            for h in range(8):
                K.dma(SP, CS_d[h, :, :], Cst[h].ap[:].rearrange("p a b -> p (a b)"), Cst[h], reads=[Cst[h]])
                K.dma(SP, NS_d[h, :, :], nst[h].ap[:], nst[h], reads=[nst[h]])
        return gen()

    XN = dscr("XN", [OWN0, D], BF16)

    def prefix_state():
        NCH = FIRST_OWN
        with Phase(K) as ph:
            mif = ph.sb([128, NCH, 16], F32, "pmif")
            for c8 in range(NCH // 8):
                K.dma(SP, mif.ap[:, c8 * 8:(c8 + 1) * 8, :], MIF.rearrange("(c p) g -> p c g", p=128)[:, c8 * 8:(c8 + 1) * 8, :], mif,
                      writes=[mif], part=(c8 > 0))
            Lx = ph.sb([128, 128], F32, "Lx")
            K.op(DVE, lambda: nc.vector.tensor_tensor(out=Lx.ap[:], in0=ones_f, in1=U_f, op=ALU.subtract), reads=[cst], writes=[Lx])
            onesb = ph.sb([128, 2], BF16, "onesb")
            K.op(DVE, lambda: nc.vector.tensor_copy(out=onesb.ap[:], in_=cst.ap[:, 256:258]), reads=[cst], writes=[onesb])
            Wl = ph.sb([128, NCH, 8], F32, "Wl")
            carry = ph.sb([128, 8], F32, "carry")
            K.op(DVE, lambda: nc.vector.memset(carry.ap[:], 0.0), writes=[carry])
            p_g = ph.ps([128, 512], F32, "ppg")
            gate_ps = Rot([T(p_g.ap[:, i * 16:(i + 1) * 16], p_g.buf) for i in range(4)])
            n_ps = [T(p_g.ap[:, 64 + i:65 + i], p_g.buf) for i in range(4)]
            p_G = [ph.ps([128, 512], F32, "ppG") for _ in range(4)]
            p_C = ph.pss(2, [128, 512], F32, "ppC")
            for c in range(NCH - 1, -1, -1):
                gp = gate_ps.next()

                def gm():
                    nc.tensor.matmul(gp.ap[:, 0:8], lhsT=Lx.ap[:], rhs=mif.ap[:, c, 8:16], start=True, stop=True)
                    return nc.tensor.matmul(gp.ap[:, 8:16], lhsT=ones_f, rhs=mif.ap[:, c, 8:16], start=True, stop=True)
                K.op(PE, gm, reads=[Lx, cst, mif], writes=[gp])
                K.op(DVE, lambda: nc.vector.tensor_tensor(out=Wl.ap[:, c, :], in0=gp.ap[:, 0:8], in1=carry.ap[:], op=ALU.add),
                     reads=[gp, carry], writes=[Wl], part=True)
                K.op(DVE, lambda: nc.vector.tensor_tensor(out=Wl.ap[:, c, :], in0=Wl.ap[:, c, :], in1=mif.ap[:, c, 0:8], op=ALU.add),
                     reads=[Wl, mif], writes=[Wl], part=True)
                K.op(DVE, lambda: nc.vector.tensor_tensor(out=carry.ap[:], in0=carry.ap[:], in1=gp.ap[:, 8:16], op=ALU.add),
                     reads=[carry, gp], writes=[carry])
            K.op(ACT, lambda: nc.scalar.activation(out=Wl.ap[:].rearrange("p c h -> p (c h)"), in_=Wl.ap[:].rearrange("p c h -> p (c h)"), func=AF.Exp),
                 reads=[Wl], writes=[Wl])
            kqs = ph.sbs(2, [128, NCH, 512], BF16, "kq")
            GTq = ph.sb([128, 32, 512], BF16, "GTq")
            xns = ph.sbs(5, [128, 4, 512], BF16, "pxn")
            wvs = ph.sbs(2, [128, 16, 512], BF16, "pwv")
            css = ph.sbs(2, [128, 512], F32, "pcs")
            nss = ph.sbs(2, [128, 4], F32, "pns")
            mkv = MK.rearrange("(c p) n -> p c n", p=128)
            wv_in = w_in.rearrange("(kc p) n -> p kc n", p=128)
            def load_kq(q):
                kq = kqs.next()
                for hh in range(6):
                    K.dma(SP, kq.ap[:, hh * 8:(hh + 1) * 8, :], mkv[:, hh * 8:(hh + 1) * 8, q * 512:(q + 1) * 512], kq, writes=[kq], part=(hh > 0))
                for c in range(NCH):
                    K.op(DVE, lambda: nc.vector.tensor_tensor(out=kq.ap[:, c, :].rearrange("p (h e) -> p h e", h=2),
                                                              in0=kq.ap[:, c, :].rearrange("p (h e) -> p h e", h=2),
                                                              in1=Wl.ap[:, c, 2 * q:2 * q + 2].unsqueeze(2).broadcast_to([128, 2, 256]), op=ALU.mult),
                         reads=[kq, Wl], writes=[kq], part=True)
                return kq
            kq_next = load_kq(0)
            for q in range(4):
                kq = kq_next
                def load_wv(hl):
                    c0 = C_MV + (2 * q + hl) * 512
                    res = []
                    for hh in range(2):
                        wv = wvs.next()
                        K.dma(POOL, wv.ap[:], wv_in[:, hh * 16:(hh + 1) * 16, c0:c0 + 512], wv, writes=[wv])
                        res.append(wv)
                    return res
                wvh = load_wv(0)
                if q + 1 < 4:
                    kq_next = load_kq(q + 1)
                def nmm():
                    ins = None
                    for hl in range(2):
                        for ec in range(2):
                            o = n_ps[hl * 2 + ec]
                            for c in range(NCH):
                                ins = nc.tensor.matmul(o.ap, lhsT=kq.ap[:, c, hl * 256 + ec * 128:hl * 256 + (ec + 1) * 128], rhs=onesb.ap[:, 0:1],
                                                       start=(c == 0), stop=(c == NCH - 1))
                    return ins
                K.op(PE, nmm, reads=[kq, onesb], writes=[n_ps[0]])
                ns_ = nss.next()
                K.op(DVE, lambda: nc.vector.tensor_copy(out=ns_.ap[:], in_=p_g.ap[:, 64:68]), reads=[n_ps[0]], writes=[ns_])
                for hl in range(2):
                    K.dma(SP, NS_d[2 * q + hl, :, :], ns_.ap[:, hl * 2:hl * 2 + 2], ns_, reads=[ns_])
                xnv = XN.rearrange("(c p) f -> p c f", p=128)
                for fbg in range(8):
                    for cg in range(NCH // 4):
                        xn = xns.next()
                        K.dma(SP if (cg % 2 == 0) else ACT, xn.ap[:], xnv[:, cg * 4:(cg + 1) * 4, fbg * 512:(fbg + 1) * 512], xn, writes=[xn])

                        def gmm():
                            ins = None
                            for j in range(4):
                                c = cg * 4 + j
                                for fl in range(4):
                                    ins = nc.tensor.matmul(p_G[fl].ap[:, :], lhsT=xn.ap[:, j, fl * 128:(fl + 1) * 128], rhs=kq.ap[:, c, :],
                                                           start=(c == 0), stop=(c == NCH - 1))
                            return ins
                        K.op(PE, gmm, reads=[xn, kq], writes=p_G, part=(cg > 0))
                    for fl in range(4):
                        evac(GTq.ap[:, fbg * 4 + fl, :], p_G[fl].ap[:, :], [p_G[fl]], [GTq], part=True)
                for hl in range(2):
                    h = 2 * q + hl
                    if hl == 1:
                        wvh = load_wv(1)
                    for ec in range(2):
                        pc = p_C.next()

                        def cmm():
                            ins = None
                            for kc in range(32):
                                ins = nc.tensor.matmul(pc.ap[:, :], lhsT=GTq.ap[:, kc, hl * 256 + ec * 128:hl * 256 + (ec + 1) * 128],
                                                       rhs=wvh[kc // 16].ap[:, kc % 16, :], start=(kc == 0), stop=(kc == 31))
                            return ins
                        K.op(PE, cmm, reads=[GTq] + wvh, writes=[pc])
                        cs_ = css.next()
                        evac(cs_.ap[:], pc.ap[:, :], [pc], [cs_])
                        K.dma(SP, CS_d[h, :, ec * 512:(ec + 1) * 512], cs_.ap[:], cs_, reads=[cs_])

    def mlstm_own():
        with Phase(K) as ph:
            ghb = ph.sb([128, D], F32, "ghb")
            K.dma(SP, ghb.ap[:], g_h.partition_broadcast(128), ghb, writes=[ghb])
            alloc_state(ph, False)
            Cb = [ph.sb([128, 2, 512], BF16, "Cb") for _ in range(8)]
            nb = [ph.sb([128, 2], BF16, "nb") for _ in range(8)]
            for h in range(8):
                K.op(ACT, lambda h=h: nc.scalar.copy(out=Cb[h].ap[:], in_=Cst[h].ap[:]), reads=[Cst[h]], writes=[Cb[h]])
                K.op(DVE, lambda h=h: nc.vector.tensor_copy(out=nb[h].ap[:], in_=nst[h].ap[:]), reads=[nst[h]], writes=[nb[h]])
            kts = ph.sbs(2, [128, 2048], BF16, "ktm")
            vts = ph.sbs(2, [128, 4096], BF16, "vtm")
            ots = ph.sbs(2, [128, 4096], BF16, "otm")
            ifs = ph.sbs(2, [128, 16], F32, "ifs")
            qTs = ph.sbs(2, [128, 16, 256], BF16, "qT")
            kTs = ph.sbs(2, [128, 16, 128], BF16, "kT")
            gw = ph.sbs(2, [128, 8], F32, "gw")
            gwb = ph.sbs(2, [128, 8], BF16, "gwb")
            geb = ph.sbs(2, [128, 8], F32, "geb")
            gdc = ph.sbs(2, [128, 8], F32, "gdc")
            gtmp = ph.sbs(2, [128, 8], F32, "gtmp")
            vsc = ph.sbs(4, [128, 512], BF16, "vsc")
            dCd = ph.sbs(1, [128, 2, 512], F32, "dCd")
            ATs = ph.sbs(3, [128, 128], BF16, "ATs")
            sml = ph.sbs(8, [128, 8], F32, "sml")
            ntmp = ph.sbs(2, [128, 2], F32, "ntmp")
            hts = ph.sbs(4, [128, 512], F32, "hts")
            jnk = ph.sb([128, 512], BF16, "mjnk")
            mls = ph.sbs(2, [128, 4096], BF16, "mls")
            mstg = ph.sb([128, 32, 512], BF16, "mstg")
            p_dC = ph.pss(1, [128, 2, 512], F32, "pdC")
            p_AT = ph.ps([128, 512], F32, "pAT")
            p_nums = ph.pss(2, [128, 512], F32, "pnum")
            p_sm = ph.ps([128, 512], F32, "psm")
            p_den = ph.ps([128, 512], F32, "pden")
            p_tr_ = ph.ps([128, 1024], BF16, "ptr")
            p_tr = Rot([T(p_tr_.ap[:, i * 512:(i + 1) * 512], p_tr_.buf) for i in range(2)])
            at_rot = Rot([T(p_AT.ap[:, i * 128:(i + 1) * 128], p_AT.buf) for i in range(4)])
            gate_ps = Rot([T(p_sm.ap[:, i * 16:(i + 1) * 16], p_sm.buf) for i in range(4)])
            dn_rot = Rot([T(p_sm.ap[:, 64 + i * 4:64 + (i + 1) * 4], p_sm.buf) for i in range(8)])
            den_rot = Rot([T(p_den.ap[:, i * 4:(i + 1) * 4], p_den.buf) for i in range(8)])
            def make_chunk(oc):
                e0 = OWN0 + oc * 128
                cx = dict(oc=oc)
                kt, vt, if_ = kts.next(), vts.next(), ifs.next()
                K.dma(SP, kt.ap[:], MK[e0:e0 + 128, :], kt, writes=[kt])
                K.dma(SP, vt.ap[:], MV[e0:e0 + 128, :], vt, writes=[vt])
                K.dma(SP, if_.ap[:], MIF[e0:e0 + 128, :], if_, writes=[if_])
                ot = ots.next()
                K.dma(SP, ot.ap[:], MO[oc * 128:(oc + 1) * 128, :], ot, writes=[ot])
                if oc % 2 == 0:
                    qk["q"] = qTs.next()
                    for hh in range(2):
                        K.dma(SP, qk["q"].ap[:, hh * 8:(hh + 1) * 8, :], MQT.rearrange("(a p) t -> p a t", p=128)[:, hh * 8:(hh + 1) * 8, oc * 128:oc * 128 + 256],
                              qk["q"], writes=[qk["q"]], part=(hh > 0))
                kTc = kTs.next()
                for q4 in range(4):
                    pt = p_tr.next()

                    def trk():
                        ins = None
                        for jj in range(4):
                            a_ = q4 * 4 + jj
                            ins = nc.tensor.transpose(out=pt.ap[:, jj * 128:(jj + 1) * 128], in_=kt.ap[:, a_ * 128:(a_ + 1) * 128], identity=identb.ap[:])
                        return ins
                    K.op(PE, trk, reads=[kt, identb], writes=[pt])
                    evac(kTc.ap[:, q4 * 4:(q4 + 1) * 4, :], pt.ap[:, :].rearrange("p (a b) -> p a b", a=4), [pt], [kTc], part=True)
                qk["k"] = kTc
                w_, wb_, eb_, dc_, tm_ = gw.next(), gwb.next(), geb.next(), gdc.next(), gtmp.next()
                chunk_gates(gate_ps.next(), if_, w_, wb_, eb_, dc_, tm_)
                cx.update(kt=kt, vt=vt, ot=ot, qT=qk["q"], kT=qk["k"], w_=w_, wb_=wb_, eb_=eb_, dc_=dc_, ml=mls.next(),
                          tcs=slice((oc % 2) * 128, (oc % 2) * 128 + 128))
                return cx

            def stA(it):
                cx, h = it["cx"], it["h"]
                qT, kT, tcs, vt, w_, wb_ = cx["qT"], cx["kT"], cx["tcs"], cx["vt"], cx["w_"], cx["wb_"]
                vs = vsc.next()
                it["vs"] = vs
                K.op(ACT, lambda: nc.scalar.activation(out=vs.ap[:], in_=vt.ap[:, h * 512:(h + 1) * 512], func=AF.Copy, scale=w_.ap[:, h:h + 1]),
                     reads=[vt, w_], writes=[vs])
                ap_ = at_rot.next()

                def am():
                    ins = None
                    for ec in range(2):
                        ins = nc.tensor.matmul(ap_.ap, lhsT=kT.ap[:, h * 2 + ec, :], rhs=qT.ap[:, h * 2 + ec, tcs],
                                               start=(ec == 0), stop=(ec == 1))
                    return ins
                K.op(PE, am, reads=[kT, qT], writes=[ap_])
                as_ = ATs.next()
                K.op(DVE, lambda: nc.vector.tensor_tensor(out=as_.ap[:], in0=ap_.ap, in1=U_f, op=ALU.mult), reads=[ap_, cst], writes=[as_])
                dp = den_rot.next()
                pn = p_nums.next()
                it["dp"], it["pn"] = dp, pn

                def nm():
                    for ec in range(2):
                        nc.tensor.matmul(pn.ap[:, :], lhsT=qT.ap[:, h * 2 + ec, tcs], rhs=Cb[h].ap[:, ec, :], start=(ec == 0), stop=False)
                    nc.tensor.matmul(pn.ap[:, :], lhsT=as_.ap[:], rhs=vs.ap[:], start=False, stop=True)
                    for ec in range(2):
                        nc.tensor.matmul(dp.ap[:, 0:1], lhsT=qT.ap[:, h * 2 + ec, tcs], rhs=nb[h].ap[:, ec:ec + 1], start=(ec == 0), stop=False)
                    return nc.tensor.matmul(dp.ap[:, 0:1], lhsT=as_.ap[:], rhs=wb_.ap[:, h:h + 1], start=False, stop=True)
                K.op(PE, nm, reads=[qT, Cb[h], nb[h], as_, vs, wb_], writes=[pn, dp])

            def stB(it):
                cx, h, dp, pn = it["cx"], it["h"], it["dp"], it["pn"]
                eb_ = cx["eb_"]
                s4 = sml.next()
                ht = hts.next()
                it["s4"], it["ht"] = s4, ht
                K.op(DVE, lambda: nc.vector.tensor_tensor(out=s4.ap[:, 0:1], in0=dp.ap[:, 0:1], in1=eb_.ap[:, h:h + 1], op=ALU.mult),
                     reads=[dp, eb_], writes=[s4])
                K.op(DVE, lambda: nc.vector.tensor_scalar(out=s4.ap[:, 1:2], in0=s4.ap[:, 0:1], scalar1=1.0, scalar2=None, op0=ALU.max),
                     reads=[s4], writes=[s4])
                K.op(DVE, lambda: nc.vector.tensor_scalar(out=s4.ap[:, 2:3], in0=s4.ap[:, 0:1], scalar1=-1.0, scalar2=1.0, op0=ALU.mult, op1=ALU.max),
                     reads=[s4], writes=[s4])
                K.op(DVE, lambda: nc.vector.tensor_tensor(out=s4.ap[:, 3:4], in0=s4.ap[:, 1:2], in1=s4.ap[:, 2:3], op=ALU.max),
                     reads=[s4], writes=[s4])
                K.op(DVE, lambda: nc.vector.reciprocal(out=s4.ap[:, 3:4], in_=s4.ap[:, 3:4]), reads=[s4], writes=[s4])
                K.op(DVE, lambda: nc.vector.tensor_tensor(out=s4.ap[:, 4:5], in0=eb_.ap[:, h:h + 1], in1=s4.ap[:, 3:4], op=ALU.mult),
                     reads=[s4, eb_], writes=[s4])
                K.op(ACT, lambda: nc.scalar.activation(out=jnk.ap[:], in_=pn.ap[:, :], func=AF.Square, scale=s4.ap[:, 4:5],
                                                       accum_out=s4.ap[:, 5:6]), reads=[pn, s4], writes=[jnk, s4])
                K.op(DVE, lambda: nc.vector.scalar_tensor_tensor(out=ht.ap[:], in0=pn.ap[:, :], scalar=s4.ap[:, 4:5],
                                                                 in1=ghb.ap[:, h * 512:(h + 1) * 512], op0=ALU.mult, op1=ALU.mult),
                     reads=[pn, s4, ghb], writes=[ht])
                K.op(DVE, lambda: nc.vector.tensor_scalar(out=s4.ap[:, 6:7], in0=s4.ap[:, 5:6], scalar1=1.0 / 512.0, scalar2=EPS,
                                                          op0=ALU.mult, op1=ALU.add), reads=[s4], writes=[s4])

            def stC(it):
                s4 = it["s4"]
                K.op(ACT, lambda: nc.scalar.activation(out=s4.ap[:, 6:7], in_=s4.ap[:, 6:7], func=AF.Sqrt), reads=[s4], writes=[s4])

            def stD(it):
                cx, h, s4, ht = it["cx"], it["h"], it["s4"], it["ht"]
                ml, ot = cx["ml"], cx["ot"]
                K.op(DVE, lambda: nc.vector.reciprocal(out=s4.ap[:, 6:7], in_=s4.ap[:, 6:7]), reads=[s4], writes=[s4])
                K.op(DVE, lambda: nc.vector.scalar_tensor_tensor(out=ml.ap[:, h * 512:(h + 1) * 512], in0=ht.ap[:], scalar=s4.ap[:, 6:7],
                                                                 in1=ot.ap[:, h * 512:(h + 1) * 512], op0=ALU.mult, op1=ALU.mult),
                     reads=[ht, s4, ot], writes=[ml], part=True)
                if h == 7:
                    finish_chunk(cx)

            def stE(it):
                cx, h = it["cx"], it["h"]
                state_update(h, cx["kt"], it["vs"], cx["wb_"], cx["dc_"], p_dC.next(), dn_rot.next(), dCd.next(), ntmp.next(), Cb, nb)

            def finish_chunk(cx):
                oc, ml = cx["oc"], cx["ml"]
                sub = oc % 4
                for q4 in range(8):
                    pt = p_tr.next()

                    def tr():
                        ins = None
                        for jj in range(4):
                            kc = q4 * 4 + jj
                            ins = nc.tensor.transpose(out=pt.ap[:, jj * 128:(jj + 1) * 128], in_=ml.ap[:, kc * 128:(kc + 1) * 128], identity=identb.ap[:])
                        return ins
                    K.op(PE, tr, reads=[ml, identb], writes=[pt])
                    evac(mstg.ap[:, q4 * 4:(q4 + 1) * 4, sub * 128:(sub + 1) * 128], pt.ap[:, :].rearrange("p (a b) -> p a b", a=4),
                         [pt], [mstg], part=True)
                if sub == 3:
                    t0 = (oc - 3) * 128
                    mv = MT.rearrange("(kc p) t -> p kc t", p=128)
                    for hh in range(2):
                        K.dma(SP, mv[:, hh * 16:(hh + 1) * 16, t0:t0 + 512], mstg.ap[:, hh * 16:(hh + 1) * 16, :], mstg, reads=[mstg])

            qk = {}
            items = []
            nit = (NOWN // 128) * 8
            stages = ((stA, 0), (stB, 1), (stE, 1), (stC, 2), (stD, 3))
            for t in range(nit + 3):
                if t < nit and t % 8 == 0:
                    cxc = make_chunk(t // 8)
                if t < nit:
                    items.append(dict(cx=cxc, h=t % 8))
                for fn, lag in stages:
                    k = t - lag
                    if 0 <= k < nit:
                        fn(items[k])

    MGT = dscr("MGT", [D, NOWN], BF16)

    def merge_tile(ti):
        o0 = ti * 1024
        with Phase(K) as ph:
            G = Gemm(ph, 32, 1024)
            wa = ph.sbs(2, [128, 8, 512], BF16, "wa")
            ATa = ph.sb([128, 8, 1024], BF16, "ATa")
            ATm = ph.sb([128, 32, 1024], BF16, "ATm")
            K.dma(SP, ATa.ap[:], ATT.rearrange("(kc p) t -> p kc t", p=128)[:, :, o0:o0 + 1024], ATa, writes=[ATa])
            mv = MT.rearrange("(kc p) t -> p kc t", p=128)
            for h in range(4):
                K.dma(SP, ATm.ap[:, h * 8:(h + 1) * 8, :], mv[:, h * 8:(h + 1) * 8, o0:o0 + 1024], ATm, writes=[ATm], part=(h > 0))
            gas = ph.sbs(8, [128, 1024], BF16, "ga")
            gms = ph.sbs(8, [128, 1024], BF16, "gm")
            waq = []
            t1s = ph.sbs(2, [128, 512], F32, "t1")
            t2s = ph.sbs(2, [128, 512], F32, "t2")
            mgs = ph.sbs(3, [128, 1024], BF16, "mgs")
            panels = []
            for pb in range(8):
                def pre(pb=pb):
                    wap = wa.next()
                    K.dma(POOL, wap.ap[:], w_ab.rearrange("(kc p) n -> p kc n", p=128)[:, :, pb * 512:(pb + 1) * 512], wap, writes=[wap])
                    gl = []
                    for cb in range(4):
                        cg = pb * 4 + cb
                        ga, gm_ = gas.next(), gms.next()
                        K.dma(SP, ga.ap[:], GT[cg * 128:(cg + 1) * 128, o0:o0 + 1024], ga, writes=[ga])
                        K.dma(SP, gm_.ap[:], GT[D + cg * 128:D + (cg + 1) * 128, o0:o0 + 1024], gm_, writes=[gm_])
                        gl.append((ga, gm_))
                    waq.append((wap, gl))

                def jobs(wp, pb=pb):
                    wap, gl = waq.pop(0)
                    for cb in range(4):
                        cg = pb * 4 + cb
                        ga, gm_ = gl[cb]
                        mg = mgs.next()
                        for tg in range(2):
                            tsl = slice(tg * 512, (tg + 1) * 512)
                            psm = G.group([(lambda p: p.ap[:, :], wp.ap[:, kc, cb * 128:(cb + 1) * 128], ATm.ap[:, kc, tsl]) for kc in range(32)], [ATm, wp])
                            psa = G.group([(lambda p: p.ap[:, :], wap.ap[:, kc, cb * 128:(cb + 1) * 128], ATa.ap[:, kc, tsl]) for kc in range(8)], [ATa, wap])
                            t1, t2 = t1s.next(), t2s.next()
                            K.op(DVE, lambda: nc.vector.tensor_tensor(out=t1.ap[:], in0=psm.ap[:, :], in1=gm_.ap[:, tsl], op=ALU.mult), reads=[psm, gm_], writes=[t1])
                            K.op(DVE, lambda: nc.vector.tensor_tensor(out=t2.ap[:], in0=psa.ap[:, :], in1=ga.ap[:, tsl], op=ALU.mult), reads=[psa, ga], writes=[t2])
                            K.op(DVE, lambda: nc.vector.tensor_tensor(out=mg.ap[:, tsl], in0=t1.ap[:], in1=t2.ap[:], op=ALU.add),
                                 reads=[t1, t2], writes=[mg], part=True)
                        K.dma(SP, MGT[cg * 128:(cg + 1) * 128, o0:o0 + 1024], mg.ap[:], mg, reads=[mg])
                panels.append(dict(w=w_mb, c0=pb * 512, ncols=512, jobs=jobs, pre=pre))
            G.run(panels)

    def outproj_tile(ti):
        o0 = ti * 1024
        with Phase(K) as ph:
            G = Gemm(ph, 32, 1024)
            MG = ph.sb([128, 32, 1024], BF16, "MG")
            gv = MGT.rearrange("(kc p) t -> p kc t", p=128)
            for h in range(4):
                K.dma(SP, MG.ap[:, h * 8:(h + 1) * 8, :], gv[:, h * 8:(h + 1) * 8, o0:o0 + 1024], MG, writes=[MG], part=(h > 0))
            xr = ph.sbs(4, [128, 512], F32, "xr")
            hs = ph.sbs(4, [128, 512], F32, "hs")
            panels = []
            for pb in range(8):
                def jobs(wp, pb=pb):
                    for ts in range(8):
                        x_ = xr.next()
                        r0 = o0 + ts * 128
                        K.dma(SP, x_.ap[:], x_ext[OWN0 + r0:OWN0 + r0 + 128, pb * 512:(pb + 1) * 512], x_, writes=[x_])
                        ps = G.group([(lambda p: p.ap[:, :], MG.ap[:, kc, ts * 128:(ts + 1) * 128], wp.ap[:, kc, :]) for kc in range(32)], [MG, wp])
                        h_ = hs.next()
                        K.op(DVE, lambda: nc.vector.tensor_tensor(out=h_.ap[:], in0=ps.ap[:, :], in1=x_.ap[:], op=ALU.add), reads=[ps, x_], writes=[h_])
                        K.dma(SP, H1[r0:r0 + 128, pb * 512:(pb + 1) * 512], h_.ap[:], h_, reads=[h_])
                panels.append(dict(w=w_out, c0=pb * 512, ncols=512, jobs=jobs))
            G.run(panels)

    def up_tile(ti):
        o0 = ti * 1024
        with Phase(K) as ph:
            G = Gemm(ph, 32, 1024)
            AT = ph.sb([128, 32, 1024], BF16, "ATu")
            hv_ = HT.rearrange("(kc p) t -> p kc t", p=128)
            for h in range(4):
                K.dma(SP, AT.ap[:, h * 8:(h + 1) * 8, :], hv_[:, h * 8:(h + 1) * 8, o0:o0 + 1024], AT, writes=[AT], part=(h > 0))
            rl = ph.sbs(3, [128, 512], F32, "rl")
            stg = ph.sbs(3, [128, 1024], BF16, "ustg")
            panels = []
            for pb in range(32):
                def jobs(wp, pb=pb):
                    for cb in range(4):
                        st = stg.next()
                        for tg in range(2):
                            ps = G.group([(lambda p: p.ap[:, :], wp.ap[:, kc, cb * 128:(cb + 1) * 128], AT.ap[:, kc, tg * 512:(tg + 1) * 512])
                                          for kc in range(32)], [AT, wp])
                            r_ = rl.next()
                            K.op(ACT, lambda: nc.scalar.activation(out=r_.ap[:], in_=ps.ap[:, :], func=AF.Relu), reads=[ps], writes=[r_])
                            K.op(DVE, lambda: nc.vector.tensor_tensor(out=st.ap[:, tg * 512:(tg + 1) * 512], in0=r_.ap[:], in1=r_.ap[:], op=ALU.mult),
                                 reads=[r_], writes=[st], part=True)
                        row0 = (pb * 4 + cb) * 128
                        K.dma(SP, UT[row0:row0 + 128, o0:o0 + 1024], st.ap[:], st, reads=[st])
                panels.append(dict(w=w_up, c0=pb * 512, ncols=512, jobs=jobs))
            G.run(panels)

    def down_tile(ti):
        o0 = ti * 512
        with Phase(K) as ph:
            AT = ph.sb([128, 128, 512], BF16, "ATd")
            uv = UT.rearrange("(kc p) t -> p kc t", p=128)
            for h in range(8):
                K.dma(SP, AT.ap[:, h * 16:(h + 1) * 16, :], uv[:, h * 16:(h + 1) * 16, o0:o0 + 512], AT, writes=[AT], part=(h > 0))
            wps = ph.sbs(3, [128, 16, 512], BF16, "wd")
            pss = ph.pss(8, [128, 512], F32, "dps")
            hr = ph.sbs(3, [128, 512], F32, "hr")
            hs = ph.sbs(3, [128, 512], F32, "hs2")
            wv = w_down.rearrange("(kc p) n -> p kc n", p=128)
            seq = [(pb, sp) for pb in range(8) for sp in range(8)]

            def load(i):
                pb, sp = seq[i]
                wp = wps.next()
                K.dma(POOL, wp.ap[:], wv[:, sp * 16:(sp + 1) * 16, pb * 512:(pb + 1) * 512], wp, writes=[wp])
                return wp
            nxt = load(0)
            for i, (pb, sp) in enumerate(seq):
                cur = nxt
                if i + 1 < len(seq):
                    nxt = load(i + 1)
                if sp == 0:
                    pst = [pss.next() for _ in range(4)]
                for ts in range(4):
                    ps = pst[ts]

                    def f():
                        ins = None
                        for kk in range(16):
                            kc = sp * 16 + kk
                            ins = nc.tensor.matmul(ps.ap[:, :], lhsT=AT.ap[:, kc, ts * 128:(ts + 1) * 128], rhs=cur.ap[:, kk, :],
                                                   start=(kc == 0), stop=(kc == 127))
                        return ins
                    K.op(PE, f, reads=[AT, cur], writes=[ps], part=(sp > 0))
                if sp == 7:
                    for ts in range(4):
                        r0 = o0 + ts * 128
                        h_ = hr.next()
                        K.dma(SP, h_.ap[:], H1[r0:r0 + 128, pb * 512:(pb + 1) * 512], h_, writes=[h_])
                        o_ = hs.next()
                        K.op(DVE, lambda: nc.vector.tensor_tensor(out=o_.ap[:], in0=pst[ts].ap[:, :], in1=h_.ap[:], op=ALU.add),
                             reads=[pst[ts], h_], writes=[o_])
                        K.dma(SP, H2[r0:r0 + 128, pb * 512:(pb + 1) * 512], o_.ap[:], o_, reads=[o_])

    def finish():
        K.barrier()
        return nc

    norm_pass(x_ext, NEXT, g_mix, dstT=XT, dst_tm=XN, ntm=OWN0 // 128)
    if stop_after <= 1:
        return finish()
    proj_phase([0, 1, 2, 3, 4, 5, 6, 7], nat=2)
    prefix_state()
    if stop_after <= 2:
        return finish()
    attention()
    if stop_after <= 3:
        return finish()
    mlstm_own()
    if stop_after <= 4:
        return finish()
    for ti in range(2):
        merge_tile(ti)
    for ti in range(2):
        outproj_tile(ti)
    if stop_after <= 5:
        return finish()
    norm_pass(H1, NOWN, g_mlp, dstT=HT)
    for ti in range(2):
        up_tile(ti)
    if stop_after <= 7:
        return finish()
    for ti in range(4):
        down_tile(ti)
    norm_pass(H2, NOWN, g_fin, dst=out_d)
    return finish()


def make_in_maps(inputs):
    x = np.asarray(inputs["x"], np.float32)
    cst, eb = _consts()
    shared = {
        "w_in": np.ascontiguousarray(inputs["w_in"][0], np.float32),
        "w_gate": np.ascontiguousarray(inputs["w_gate"][0], np.float32),
        "w_ab": np.ascontiguousarray(inputs["w_attn_branch"][0], np.float32),
        "w_mb": np.ascontiguousarray(inputs["w_mlstm_branch"][0], np.float32),
        "w_out": np.ascontiguousarray(inputs["w_out"][0], np.float32),
        "w_up": np.ascontiguousarray(inputs["w_up"][0], np.float32),
        "w_down": np.ascontiguousarray(inputs["w_down"][0], np.float32),
        "g_mix": np.ascontiguousarray(inputs["norm_mix_g"][0], np.float32),
        "g_mlp": np.ascontiguousarray(inputs["norm_mlp_g"][0], np.float32),
        "g_fin": np.ascontiguousarray(inputs["norm_final_g"], np.float32),
        "g_h": np.ascontiguousarray(inputs["mlstm_norm_g"][0], np.float32),
        "b_g": np.ascontiguousarray(np.asarray(inputs["b_gate"][0], np.float32).reshape(64, 128).T),
        "b_if": np.ascontiguousarray(np.concatenate([np.asarray(inputs["b_igate"][0], np.float32),
                                                      np.asarray(inputs["b_fgate"][0], np.float32)])),
        "cst": cst,
        "ebias": eb,
    }
    maps = []
    for core in range(8):
        b, c = core // 4, core % 4
        xe = np.zeros((NEXT, D), np.float32)
        n = (c + 1) * NOWN
        xe[NEXT - n:] = x[b, :n]
        m = dict(shared)
        m["x_ext"] = xe
        m["halo_valid"] = np.full((128, 1), 0.0 if c == 0 else 1.0, np.float32)
        maps.append(m)
    return maps


def kernel(**inputs):
    nc = build_program()
    maps = make_in_maps(inputs)
    res = run_bass_kernel_spmd(nc, maps, core_ids=list(range(8)))
    out = np.zeros((2, SEQ, D), np.float32)
    for core in range(8):
        b, c = core // 4, core % 4
        out[b, c * NOWN:(c + 1) * NOWN] = res.results[core]["out"]
    return out
```

```python
import math
from contextlib import ExitStack
import numpy as np
import concourse.bass as bass
import concourse.mybir as mybir
from concourse.bass_utils import run_bass_kernel_spmd

F32, BF16 = mybir.dt.float32, mybir.dt.bfloat16
AF = mybir.ActivationFunctionType
ALU = mybir.AluOpType

D = 4096
SEQ = 8192
NOWN = 2048
NEXT = 8192
OWN0 = NEXT - NOWN
HALO0 = OWN0 - 2048
IN_W = 21520
C_AQ, C_AK, C_AV, C_MQ, C_MK, C_MV, C_MO, C_MI = 0, 3072, 6144, 9216, 11264, 13312, 17408, 21504
DFF = 16384
EPS = 1e-6
GDIL = (1, 4, 16)


class Eng:
    def __init__(self, K, h, name):
        self.h = h
        self.sem = K.new_sem(name)
        self.cnt = 0
        self.seen = {}
        self.name = name

    def wait(self, ev):
        sem, val = ev
        k = id(sem)
        if sem is self.sem and self.name == "pe":
            return
        if self.seen.get(k, 0) >= val:
            return
        self.h.wait_ge(sem, val)
        self.seen[k] = val


class Buf:
    __slots__ = ("w", "r", "ds", "excl")

    def __init__(self):
        self.w = {}
        self.r = {}
        self.ds = None
        self.excl = False


class T:
    __slots__ = ("ap", "buf")

    def __init__(self, ap, buf=None):
        self.ap = ap
        self.buf = buf if buf is not None else Buf()

    def __getitem__(self, idx):
        return self.ap[idx]


class Rot:
    def __init__(self, items):
        self.items = items
        self.i = 0

    def next(self):
        t = self.items[self.i % len(self.items)]
        self.i += 1
        return t


class Kern:
    def __init__(self, nc):
        self.nc = nc
        self.gs = ExitStack()
        self.nsem = 0
        self.uid = 0
        self.PE = Eng(self, nc.tensor, "pe")
        self.ACT = Eng(self, nc.scalar, "act")
        self.DVE = Eng(self, nc.vector, "dve")
        self.POOL = Eng(self, nc.gpsimd, "pool")
        self.SP = Eng(self, nc.sync, "sp")
        self.engs = [self.PE, self.ACT, self.DVE, self.POOL, self.SP]
        self.dpool = []
        self.dlive = []
        self.phase_bufs = []

    def new_sem(self, name):
        self.nsem += 1
        return self.gs.enter_context(self.nc.semaphore(f"{name}_{self.nsem}"))

    def name(self, p):
        self.uid += 1
        return f"{p}{self.uid}"

    def _pre(self, eng, reads, writes):
        for t in reads:
            for ev in t.buf.w.values():
                eng.wait(ev)
            if t.buf.excl:
                for ev in t.buf.r.values():
                    if ev[0] is not eng.sem:
                        eng.wait(ev)
        for t in writes:
            for ev in t.buf.w.values():
                eng.wait(ev)
            for ev in t.buf.r.values():
                eng.wait(ev)

    def _post(self, ev, reads, writes, part=False):
        k = id(ev[0])
        for t in reads:
            t.buf.r[k] = ev
        for t in writes:
            if part:
                t.buf.w[k] = ev
            else:
                t.buf.w = {k: ev}
                t.buf.r = {}

    def op(self, eng, fn, reads=(), writes=(), part=False):
        self._pre(eng, reads, writes)
        ins = fn()
        eng.cnt += 1
        ins.then_inc(eng.sem, 1)
        self._post((eng.sem, eng.cnt), reads, writes, part)

    def dma(self, eng, out, in_, sb, reads=(), writes=(), part=False, **kw):
        b = sb.buf
        if b.ds is None:
            if self.dpool:
                b.ds = self.dpool.pop()
            else:
                b.ds = [self.new_sem("d"), 0]
                self.dlive.append(b.ds)
            self.phase_bufs.append(b)
        self._pre(eng, reads, writes)
        ins = eng.h.dma_start(out=out, in_=in_, **kw)
        b.ds[1] += 16
        ins.then_inc(b.ds[0], 16)
        self._post((b.ds[0], b.ds[1]), reads, writes, part)

    def barrier(self, engs=None):
        evs = [(e.sem, e.cnt) for e in self.engs if e.cnt > 0]
        evs += [(d[0], d[1]) for d in self.dlive if d[1] > 0]
        for e in (engs or self.engs):
            for ev in evs:
                e.wait(ev)

    def end_phase(self):
        self.barrier()
        for b in self.phase_bufs:
            self.dpool.append(b.ds)
            b.ds = None
        self.phase_bufs = []


class Phase:
    def __init__(self, K):
        self.K = K
        self.st = ExitStack()

    def __enter__(self):
        self.st.__enter__()
        return self

    def __exit__(self, *a):
        self.K.end_phase()
        return self.st.__exit__(*a)

    def sb(self, shape, dt, nm="t"):
        return T(self.st.enter_context(self.K.nc.sbuf_tensor(self.K.name(nm), list(shape), dt)))

    def ps(self, shape, dt, nm="p"):
        t = T(self.st.enter_context(self.K.nc.psum_tensor(self.K.name(nm), list(shape), dt)))
        t.buf.excl = True
        return t

    def sbs(self, n, shape, dt, nm="t"):
        return Rot([self.sb(shape, dt, nm) for _ in range(n)])

    def pss(self, n, shape, dt, nm="p"):
        return Rot([self.ps(shape, dt, nm) for _ in range(n)])


def _alibi_slope_list(n):
    def pow2(m):
        start = 2.0 ** (-(2.0 ** -(math.log2(m) - 3)))
        return [start ** (i + 1) for i in range(m)]
    if math.log2(n).is_integer():
        return pow2(n)
    c = 2 ** math.floor(math.log2(n))
    return pow2(c) + _alibi_slope_list(2 * c)[0::2][: n - c]


def _consts():
    s = np.asarray(sorted(_alibi_slope_list(24), reverse=True), dtype=np.float32).reshape(3, 8)
    cst = np.zeros((128, 384), np.float32)
    cst[:, 0:128] = np.eye(128, dtype=np.float32)
    i = np.arange(128)
    cst[:, 128:256] = (i[:, None] <= i[None, :]).astype(np.float32)
    cst[:, 256:384] = 1.0
    eb = np.full((128, 24, 256), -30000.0, np.float32)
    k = i[:, None].astype(np.float32)
    q = i[None, :].astype(np.float32)
    for g in range(3):
        for j in range(8):
            relp = q - k + 128.0
            rels = q - k
            bp = np.where(relp <= 128.0, -s[g, j] * relp * GDIL[g], -30000.0)
            bs = np.where(rels >= 0.0, -s[g, j] * rels * GDIL[g], -30000.0)
            eb[:, g * 8 + j, 0:128] = bp
            eb[:, g * 8 + j, 128:256] = bs
    return cst, eb.reshape(128, 24 * 256)


def build_program(dbg=(), stop_after=99):
    nc = bass.Bass("TRN2", target_bir_lowering=False)
    K = Kern(nc)
    PE, ACT, DVE, POOL, SP = K.PE, K.ACT, K.DVE, K.POOL, K.SP

    def din(name, shape, dt=F32):
        return nc.dram_tensor(name, list(shape), dt, kind="ExternalInput").ap()

    def dscr(name, shape, dt):
        kind = "ExternalOutput" if name in dbg else "Internal"
        return nc.dram_tensor(name, list(shape), dt, kind=kind).ap()

    x_ext = din("x_ext", [NEXT, D])
    w_in = din("w_in", [D, IN_W])
    w_gate = din("w_gate", [D, 2 * D])
    w_ab = din("w_ab", [1024, D])
    w_mb = din("w_mb", [D, D])
    w_out = din("w_out", [D, D])
    w_up = din("w_up", [D, DFF])
    w_down = din("w_down", [DFF, D])
    g_mix = din("g_mix", [D])
    g_mlp = din("g_mlp", [D])
    g_fin = din("g_fin", [D])
    g_h = din("g_h", [D])
    b_g = din("b_g", [128, 64])
    b_if = din("b_if", [16])
    cst_d = din("cst", [128, 384])
    eb_d = din("ebias", [128, 24 * 256])
    hv_d = din("halo_valid", [128, 1])
    out_d = nc.dram_tensor("out", [NOWN, D], F32, kind="ExternalOutput").ap()

    XT = dscr("XT", [D, NEXT], BF16)
    AQT = dscr("AQT", [3, 1024, NOWN], BF16)
    AKT = dscr("AKT", [3, 1024, NOWN], BF16)
    AKH = dscr("AKH", [3, 1024, 2048], BF16)
    AV = dscr("AV", [3, NOWN, 1024], BF16)
    AVH = dscr("AVH", [3, 2048, 1024], BF16)
    MQT = dscr("MQT", [2048, NOWN], BF16)
    MKT = dscr("MKT", [2048, NOWN], BF16)
    MK = dscr("MK", [NEXT, 2048], BF16)
    MV = dscr("MV", [NEXT, 4096], BF16)
    MO = dscr("MO", [NOWN, 4096], BF16)
    MIF = dscr("MIF", [NEXT, 16], F32)
    GT = dscr("GT", [2 * D, NOWN], BF16)
    NUM = dscr("NUM", [3, NOWN, 8, 132], F32)
    ATT = dscr("ATT", [1024, NOWN], BF16)
    MT = dscr("MT", [D, NOWN], BF16)
    H1 = dscr("H1", [NOWN, D], F32)
    HT = dscr("HT", [D, NOWN], BF16)
    UT = dscr("UT", [DFF, NOWN], BF16)
    H2 = dscr("H2", [NOWN, D], F32)

    cs = K.gs
    def gsb(shape, dt, nm):
        return T(cs.enter_context(nc.sbuf_tensor(nm, list(shape), dt)))
    cst = gsb([128, 384], F32, "cst_sb")
    identb = gsb([128, 128], BF16, "identb")
    K.dma(SP, cst.ap[:], cst_d[:, :], cst, writes=[cst])
    K.op(DVE, lambda: nc.vector.tensor_copy(out=identb.ap[:], in_=cst.ap[:, 0:128]), reads=[cst], writes=[identb])
    U_f = cst.ap[:, 128:256]
    ones_f = cst.ap[:, 256:384]

    def norm_pass(src, ntok, gvec, dstT=None, dst=None, dst_tm=None, ntm=0):
        with Phase(K) as ph:
            gbc = ph.sb([128, D], F32, "gbc")
            K.dma(SP, gbc.ap[:], gvec.partition_broadcast(128), gbc, writes=[gbc])
            xts = ph.sbs(5, [128, D], F32, "xt")
            junk = ph.sb([128, D], BF16, "junk")
            ssqs = ph.sbs(4, [128, 1], F32, "ssq")
            rstds = ph.sbs(4, [128, 1], F32, "rstd")
            if dstT is not None:
                xns = ph.sbs(3, [128, D], BF16, "xn")
                stgs = ph.sbs(2, [128, 32, 512], BF16, "stg")
                pts = ph.pss(6, [128, 512], BF16, "pt")
            else:
                ys = ph.sbs(3, [128, D], F32, "y")
            nt = ntok // 128
            loaded = {}
            st_ = {}

            def load(i):
                xt = xts.next()
                K.dma(SP, xt.ap[:], src[i * 128:(i + 1) * 128, :], xt, writes=[xt])
                loaded[i] = xt

            def S0(i):
                if i + 2 < nt:
                    load(i + 2)
                xt = loaded.pop(i)
                ssq = ssqs.next()
                rstd = rstds.next()
                st_[i] = dict(xt=xt, ssq=ssq, rstd=rstd)
                K.op(ACT, lambda: nc.scalar.activation(out=junk.ap[:], in_=xt.ap[:], func=AF.Square, accum_out=ssq.ap[:]),
                     reads=[xt], writes=[junk, ssq])
                K.op(DVE, lambda: nc.vector.tensor_scalar(out=rstd.ap[:], in0=ssq.ap[:], scalar1=1.0 / D, scalar2=EPS,
                                                          op0=ALU.mult, op1=ALU.add), reads=[ssq], writes=[rstd])

            def S1(i):
                rstd = st_[i]["rstd"]
                K.op(ACT, lambda: nc.scalar.activation(out=rstd.ap[:], in_=rstd.ap[:], func=AF.Sqrt), reads=[rstd], writes=[rstd])

            def S2(i):
                d = st_[i]
                xt, rstd = d["xt"], d["rstd"]
                K.op(DVE, lambda: nc.vector.reciprocal(out=rstd.ap[:], in_=rstd.ap[:]), reads=[rstd], writes=[rstd])
                if dstT is None:
                    y = ys.next()
                    K.op(DVE, lambda: nc.vector.scalar_tensor_tensor(out=y.ap[:], in0=xt.ap[:], scalar=rstd.ap[:, 0:1], in1=gbc.ap[:],
                                                                     op0=ALU.mult, op1=ALU.mult), reads=[xt, rstd, gbc], writes=[y])
                    K.dma(SP, dst[i * 128:(i + 1) * 128, :], y.ap[:], y, reads=[y])
                    return
                xn = xns.next()
                d["xn"] = xn
                K.op(DVE, lambda: nc.vector.scalar_tensor_tensor(out=xn.ap[:], in0=xt.ap[:], scalar=rstd.ap[:, 0:1], in1=gbc.ap[:],
                                                                 op0=ALU.mult, op1=ALU.mult), reads=[xt, rstd, gbc], writes=[xn])
                if dst_tm is not None and i < ntm:
                    K.dma(SP, dst_tm[i * 128:(i + 1) * 128, :], xn.ap[:], xn, reads=[xn])

            cur = {"stg": None, "ev": 0}

            def S3(i):
                if dstT is None:
                    st_.pop(i)
                    return
                xn = st_.pop(i)["xn"]
                if i % 4 == 0:
                    cur["stg"] = stgs.next()
                stg = cur["stg"]
                sub = i % 4
                for q4 in range(8):
                    pt = pts.next()

                    def tr():
                        ins = None
                        for jj in range(4):
                            kc = q4 * 4 + jj
                            ins = nc.tensor.transpose(out=pt.ap[:, jj * 128:(jj + 1) * 128], in_=xn.ap[:, kc * 128:(kc + 1) * 128],
                                                      identity=identb.ap[:])
                        return ins
                    K.op(PE, tr, reads=[xn, identb], writes=[pt])
                    dsto = stg.ap[:, q4 * 4:(q4 + 1) * 4, sub * 128:(sub + 1) * 128]
                    srci = pt.ap[:, :].rearrange("p (a b) -> p a b", a=4)
                    if cur["ev"] % 2 == 0:
                        K.op(ACT, lambda: nc.scalar.copy(out=dsto, in_=srci), reads=[pt], writes=[stg], part=True)
                    else:
                        K.op(DVE, lambda: nc.vector.tensor_copy(out=dsto, in_=srci), reads=[pt], writes=[stg], part=True)
                    cur["ev"] += 1
                if sub == 3:
                    t0 = (i - 3) * 128
                    dv = dstT.rearrange("(kc p) t -> p kc t", p=128)
                    for h in range(2):
                        K.dma(SP, dv[:, h * 16:(h + 1) * 16, t0:t0 + 512], stg.ap[:, h * 16:(h + 1) * 16, :], stg, reads=[stg])

            load(0)
            if nt > 1:
                load(1)
            for step in range(nt + 3):
                for lag, fn in enumerate((S0, S1, S2, S3)):
                    i = step - lag
                    if 0 <= i < nt:
                        fn(i)

    class Gemm:
        def __init__(self, ph, KC, Tn, npan=2):
            self.ph, self.KC, self.Tn = ph, KC, Tn
            self.wps = ph.sbs(npan, [128, KC, 512], BF16, "wp")
            self.pss = ph.pss(4, [128, 512], F32, "gps")
            self.tick = None

        def load_panel(self, w, c0, ncols, KC=None):
            KC = KC or self.KC
            wp = self.wps.next()
            wv = w.rearrange("(kc p) n -> p kc n", p=128)
            nsp = 2 if KC >= 16 else 1
            for h in range(nsp):
                k0, k1 = h * KC // nsp, (h + 1) * KC // nsp
                K.dma(POOL, wp.ap[:, k0:k1, 0:ncols], wv[:, k0:k1, c0:c0 + ncols], wp, writes=[wp], part=(h > 0))
            return wp

        def run(self, panels):
            def ld(q):
                if "pre" in q:
                    q["pre"]()
                return self.load_panel(q["w"], q["c0"], q["ncols"], q.get("KC"))
            nxt = ld(panels[0])
            for i, p in enumerate(panels):
                cur = nxt
                if i + 1 < len(panels):
                    nxt = ld(panels[i + 1])
                p["jobs"](cur)

        def group(self, mms, reads):
            ps = self.pss.next()

            def f():
                ins = None
                n = len(mms)
                for i, (o, l, r) in enumerate(mms):
                    ins = nc.tensor.matmul(o(ps), lhsT=l, rhs=r, start=(i == 0), stop=(i == n - 1))
                return ins
            K.op(PE, f, reads=reads, writes=[ps])
            if self.tick is not None:
                self.tick()
            return ps

    evq = [0]

    def evac(out_ap, in_ap, reads, writes, part=False, scale=None, func=None, bias=None, eng=None):
        use_act = (func is not None) or (eng == "act") or (eng is None and evq[0] % 2 == 0)
        evq[0] += 1
        if use_act:
            kw = {}
            if scale is not None:
                kw["scale"] = scale
            if bias is not None:
                kw["bias"] = bias
            K.op(ACT, lambda: nc.scalar.activation(out=out_ap, in_=in_ap, func=(func or AF.Copy), **kw),
                 reads=reads, writes=writes, part=part)
        else:
            if scale is not None:
                K.op(DVE, lambda: nc.vector.tensor_scalar(out=out_ap, in0=in_ap, scalar1=scale, scalar2=None, op0=ALU.mult),
                     reads=reads, writes=writes, part=part)
            else:
                K.op(DVE, lambda: nc.vector.tensor_copy(out=out_ap, in_=in_ap), reads=reads, writes=writes, part=part)

    def proj_alloc(ph, nat=1):
        R = dict(G=Gemm(ph, 32, 1024), ATs=ph.sbs(nat, [128, 32, 1024], BF16, "AT"), ATnext=None,
                 stg_tm=ph.sbs(4, [128, 512], BF16, "stm"), stg_fm=ph.sbs(2, [128, 1024], BF16, "sfm"),
                 bif=ph.sb([128, 16], F32, "bif"), bgs=ph.sb([128, 64], F32, "bgs"),
                 gsm=ph.sbs(2, [128, 16], F32, "gsm"), gsm2=ph.sbs(2, [128, 16], F32, "gsm2"))
        K.dma(SP, R["bif"].ap[:], b_if.partition_broadcast(128), R["bif"], writes=[R["bif"]])
        K.dma(SP, R["bgs"].ap[:], b_g[:, :], R["bgs"], writes=[R["bgs"]])
        return R

    def load_AT(R, ti):
        AT = R["ATs"].next()
        e0 = ti * 1024
        xv = XT.rearrange("(kc p) t -> p kc t", p=128)
        for h in range(4):
            K.dma(SP, AT.ap[:, h * 8:(h + 1) * 8, :], xv[:, h * 8:(h + 1) * 8, e0:e0 + 1024], AT, writes=[AT], part=(h > 0))
        return AT

    def proj_phase(tis, bgf=None, nat=1):
        with Phase(K) as ph:
            R = proj_alloc(ph, nat)
            bg = bgf(ph) if bgf is not None else None
            if bg is not None:
                cnt = [0]

                def tick():
                    cnt[0] += 1
                    if cnt[0] % 2 == 0:
                        next(bg, None)
                R["G"].tick = tick
            R["ATnext"] = load_AT(R, tis[0])
            for i, ti in enumerate(tis):
                R["AT"] = R["ATnext"]
                if nat > 1 and i + 1 < len(tis):
                    R["ATnext"] = load_AT(R, tis[i + 1])
                proj_tile(ti, R)
                if nat == 1 and i + 1 < len(tis):
                    R["ATnext"] = load_AT(R, tis[i + 1])
            if bg is not None:
                R["G"].tick = None
                for _ in bg:
                    pass

    def proj_tile(ti, R):
        e0 = ti * 1024
        own = e0 >= OWN0
        halo = (not own) and e0 >= HALO0
        o0 = e0 - OWN0
        u0 = e0 - HALO0
        if True:
            G, AT, stg_tm, stg_fm, bif, bgs, gsm, gsm2 = (R[k] for k in ("G", "AT", "stg_tm", "stg_fm", "bif", "bgs", "gsm", "gsm2"))
            panels = []

            def nat_cols(ts):
                return lambda kc: AT.ap[:, kc, ts * 128:(ts + 1) * 128]

            def tm_panel(c0, subtiles, dst_fn, scale=None, func=None):
                def jobs(wp):
                    for st_i, (lf, M) in enumerate(subtiles):
                        M = M or 128
                        ps = G.group([((lambda p, M=M: p.ap[0:M, :]), lf(kc), wp.ap[:, kc, :]) for kc in range(32)], [AT, wp])
                        st = stg_tm.next()
                        evac(st.ap[0:M, :], ps.ap[0:M, :], [ps], [st], scale=scale, func=func)
                        for (p0, p1, dap) in dst_fn(st_i):
                            K.dma(SP, dap, st.ap[p0:p1, :], st, reads=[st])
                panels.append(dict(w=w_in, c0=c0, ncols=512, jobs=jobs))

            def fm_panel(w, c0, perm, dst_fn, scale=None, func=None, bias_fn=None):
                def jobs(wp):
                    for cb in range(4):
                        st = stg_fm.next()
                        for tg in range(2):
                            ps = G.group([(lambda p: p.ap[:, :], wp.ap[:, kc, cb * 128:(cb + 1) * 128],
                                           AT.ap[:, kc, tg * 512:(tg + 1) * 512]) for kc in range(32)], [AT, wp])
                            if perm == 1:
                                o = st.ap[:, :]
                                o = o[:, tg * 512:(tg + 1) * 512]
                                i_ = ps.ap[:, :]
                            else:
                                d = perm
                                per = 1024 // d
                                o = st.ap[:, :].rearrange("p (r l) -> p r l", r=d)[:, :, tg * (512 // d):(tg + 1) * (512 // d)]
                                i_ = ps.ap[:, :].rearrange("p (l r) -> p r l", r=d)
                            b = bias_fn(c0 // 128 + cb) if bias_fn else None
                            evac(o, i_, [ps], [st], part=True, scale=scale, func=func, bias=b)
                        dst_fn(c0 // 128 + cb, st)
                panels.append(dict(w=w, c0=c0, ncols=512, jobs=jobs))

            def add_attn_kv(groups, is_own):
                KD = AKT if is_own else AKH
                VD = AV if is_own else AVH
                toff = o0 if is_own else u0
                for g in groups:
                    d = GDIL[g]
                    for pb in range(2):
                        cbase = C_AK + g * 1024 + pb * 512

                        def kdst(cbg, st, g=g, d=d):
                            row0 = (cbg - (C_AK + g * 1024) // 128) * 128
                            if d == 1:
                                K.dma(SP, KD[g, row0:row0 + 128, toff:toff + 1024], st.ap[:, :], st, reads=[st])
                            else:
                                per = 1024 // d
                                dv = KD[g, row0:row0 + 128, :].rearrange("p (r l) -> p r l", r=d)[:, :, toff // d:toff // d + per]
                                K.dma(SP, dv, st.ap[:, :].rearrange("p (r l) -> p r l", r=d), st, reads=[st])
                        fm_panel(w_in, cbase, d, kdst)
                    for pb in range(2):
                        cbase = C_AV + g * 1024 + pb * 512
                        if d == 1:
                            subt = [(nat_cols(ts), None) for ts in range(8)]

                            def vdst(si, pb=pb, g=g):
                                r0 = toff + si * 128
                                return [(0, 128, VD[g, r0:r0 + 128, pb * 512:(pb + 1) * 512])]
                        elif d == 4:
                            subt = []
                            for r in range(4):
                                for hl in range(2):
                                    s0 = r + 512 * hl
                                    subt.append(((lambda kc, s0=s0: AT.ap[:, kc, s0:s0 + 509:4]), None))

                            def vdst(si, pb=pb, g=g):
                                r, hl = si // 2, si % 2
                                p0 = r * 512 + toff // 4 + hl * 128
                                return [(0, 128, VD[g, p0:p0 + 128, pb * 512:(pb + 1) * 512])]
                        else:
                            subt = []
                            for rho in range(16):
                                subt.append(((lambda kc, rho=rho: AT.ap[:, kc, rho:rho + 16 * 63 + 1:16]), 64))

                            def vdst(si, pb=pb, g=g):
                                p0 = si * 128 + toff // 16
                                return [(0, 64, VD[g, p0:p0 + 64, pb * 512:(pb + 1) * 512])]
                        tm_panel(cbase, subt, vdst)

            nat8 = [(nat_cols(ts), None) for ts in range(8)]
            if own:
                for g in range(3):
                    d = GDIL[g]
                    for pb in range(2):
                        def qdst(cbg, st, g=g, d=d):
                            row0 = (cbg - (g * 1024) // 128) * 128
                            if d == 1:
                                K.dma(SP, AQT[g, row0:row0 + 128, o0:o0 + 1024], st.ap[:, :], st, reads=[st])
                            else:
                                per = 1024 // d
                                dv = AQT[g, row0:row0 + 128, :].rearrange("p (r l) -> p r l", r=d)[:, :, o0 // d:o0 // d + per]
                                K.dma(SP, dv, st.ap[:, :].rearrange("p (r l) -> p r l", r=d), st, reads=[st])
                        fm_panel(w_in, C_AQ + g * 1024 + pb * 512, d, qdst)
                add_attn_kv([0, 1, 2], True)
                for sec, DT, sc in ((C_MQ, MQT, None),):
                    for pb in range(4):
                        def mdst(cbg, st, sec=sec, DT=DT):
                            row0 = (cbg - sec // 128) * 128
                            K.dma(SP, DT[row0:row0 + 128, o0:o0 + 1024], st.ap[:, :], st, reads=[st])
                        fm_panel(w_in, sec + pb * 512, 1, mdst, scale=sc)
                for pb in range(16):
                    def gdst(cbg, st):
                        K.dma(SP, GT[cbg * 128:(cbg + 1) * 128, o0:o0 + 1024], st.ap[:, :], st, reads=[st])
                    fm_panel(w_gate, pb * 512, 1, gdst, func=AF.Sigmoid, bias_fn=lambda cbg: bgs.ap[:, cbg:cbg + 1])
                for pb in range(8):
                    tm_panel(C_MO + pb * 512, nat8,
                             (lambda si, pb=pb: [(0, 128, MO[o0 + si * 128:o0 + (si + 1) * 128, pb * 512:(pb + 1) * 512])]),
                             func=AF.Sigmoid)
            elif halo:
                if u0 == 0:
                    add_attn_kv([2], False)
                else:
                    add_attn_kv([0, 1, 2], False)
            for pb in range(4):
                tm_panel(C_MK + pb * 512, nat8,
                         (lambda si, pb=pb: [(0, 128, MK[e0 + si * 128:e0 + (si + 1) * 128, pb * 512:(pb + 1) * 512])]),
                         scale=1.0 / 16.0)
            for pb in (range(8) if own else ()):
                tm_panel(C_MV + pb * 512, nat8,
                         (lambda si, pb=pb: [(0, 128, MV[e0 + si * 128:e0 + (si + 1) * 128, pb * 512:(pb + 1) * 512])]))

            def gate_jobs(wp):
                for ts in range(8):
                    ps = G.group([(lambda p: p.ap[:, 0:16], AT.ap[:, kc, ts * 128:(ts + 1) * 128], wp.ap[:, kc, 0:16]) for kc in range(32)],
                                 [AT, wp])
                    z = gsm.next()
                    r = gsm2.next()
                    K.op(DVE, lambda: nc.vector.tensor_tensor(out=z.ap[:], in0=ps.ap[:, 0:16], in1=bif.ap[:], op=ALU.add),
                         reads=[ps, bif], writes=[z])
                    K.op(ACT, lambda: nc.scalar.activation(out=z.ap[:], in_=z.ap[:], func=AF.Tanh, scale=1.0 / 15.0), reads=[z], writes=[z])
                    K.op(DVE, lambda: nc.vector.tensor_scalar(out=r.ap[:, 0:8], in0=z.ap[:, 0:8], scalar1=15.0, scalar2=None, op0=ALU.mult),
                         reads=[z], writes=[r], part=True)
                    K.op(ACT, lambda: nc.scalar.activation(out=z.ap[:, 8:16], in_=z.ap[:, 8:16], func=AF.Exp, scale=-15.0), reads=[z], writes=[z])
                    K.op(DVE, lambda: nc.vector.tensor_scalar(out=z.ap[:, 8:16], in0=z.ap[:, 8:16], scalar1=1.0, scalar2=None, op0=ALU.add),
                         reads=[z], writes=[z])
                    K.op(ACT, lambda: nc.scalar.activation(out=z.ap[:, 8:16], in_=z.ap[:, 8:16], func=AF.Ln), reads=[z], writes=[z])
                    K.op(DVE, lambda: nc.vector.tensor_scalar(out=r.ap[:, 8:16], in0=z.ap[:, 8:16], scalar1=-1.0, scalar2=None, op0=ALU.mult),
                         reads=[z], writes=[r], part=True)
                    K.dma(SP, MIF[e0 + ts * 128:e0 + (ts + 1) * 128, :], r.ap[:], r, reads=[r])
            panels.append(dict(w=w_in, c0=C_MI, ncols=16, jobs=gate_jobs))
            G.run(panels)

    def attention():
        with Phase(K) as ph:
            ebf = ph.sb([128, 24 * 256], F32, "ebf")
            K.dma(SP, ebf.ap[:], eb_d[:, :], ebf, writes=[ebf])
            hv = ph.sb([128, 1], F32, "hv")
            K.dma(SP, hv.ap[:], hv_d[:, :], hv, writes=[hv])
            EB = ph.sb([128, 24, 256], BF16, "EB")
            EBH = ph.sb([128, 24, 256], BF16, "EBH")
            K.op(ACT, lambda: nc.scalar.activation(out=ebf.ap[:], in_=ebf.ap[:], func=AF.Exp), reads=[ebf], writes=[ebf])
            K.op(DVE, lambda: nc.vector.tensor_copy(out=EB.ap[:].rearrange("p a b -> p (a b)"), in_=ebf.ap[:]), reads=[ebf], writes=[EB])
            K.op(DVE, lambda: nc.vector.tensor_copy(out=EBH.ap[:].rearrange("p a b -> p (a b)"), in_=ebf.ap[:]), reads=[ebf], writes=[EBH])
            K.op(DVE, lambda: nc.vector.tensor_scalar(out=EBH.ap[:, :, 0:128], in0=EBH.ap[:, :, 0:128], scalar1=hv.ap[:, 0:1], scalar2=None,
                                                      op0=ALU.mult), reads=[EBH, hv], writes=[EBH])
            QTs = ph.sbs(3, [128, 2048], BF16, "QT")
            KTs = ph.sbs(3, [128, 2048], BF16, "KT")
            KHs = ph.sbs(3, [128, 2048], BF16, "KH")
            Vs = ph.sbs(3, [128, 16, 132], BF16, "V")
            VHs = ph.sbs(3, [128, 16, 132], BF16, "VH")
            for r_ in (Vs, VHs):
                for t in r_.items:
                    K.op(DVE, lambda t=t: nc.vector.memset(t.ap[:, :, 128:132], 1.0), writes=[t])
            PTs = ph.sbs(4, [128, 256], BF16, "PT")
            PEs = ph.sbs(5, [128, 256], BF16, "PE")
            Os = ph.sbs(4, [128, 132], F32, "O")
            sps = ph.pss(4, [128, 256], F32, "sps")
            ops_ = ph.pss(4, [128, 132], F32, "ops")
            def stageP(g, j, n, tl):
                QT, KT, KH, V, VH = tl
                if g == 0:
                    hal = (n == 0)
                    pblk = 15 if hal else n - 1
                elif g == 1:
                    hal = (n % 4 == 0)
                    pblk = n + 3 if hal else n - 1
                else:
                    hal = True
                    pblk = n
                Kp, Vp = (KH, VH) if hal else (KT, V)
                sp = sps.next()

                def sc():
                    nc.tensor.matmul(sp.ap[:, 0:128], lhsT=Kp.ap[:, pblk * 128:(pblk + 1) * 128], rhs=QT.ap[:, n * 128:(n + 1) * 128],
                                     start=True, stop=True)
                    return nc.tensor.matmul(sp.ap[:, 128:256], lhsT=KT.ap[:, n * 128:(n + 1) * 128], rhs=QT.ap[:, n * 128:(n + 1) * 128],
                                            start=True, stop=True)
                K.op(PE, sc, reads=[Kp, KT, QT], writes=[sp])
                pt = PTs.next()
                pe_ = PEs.next()
                K.op(ACT, lambda: nc.scalar.activation(out=pt.ap[:], in_=sp.ap[:], func=AF.Exp, scale=1.0 / math.sqrt(128.0)),
                     reads=[sp], writes=[pt])
                tab = EBH if hal else EB
                K.op(DVE, lambda: nc.vector.tensor_tensor(out=pe_.ap[:], in0=pt.ap[:], in1=tab.ap[:, g * 8 + j, :], op=ALU.mult),
                     reads=[pt, tab], writes=[pe_])
                return (g, j, n, pe_, Vp, V, pblk)

            def stageQ(item):
                g, j, n, pe_, Vp, V, pblk = item
                op_ = ops_.next()

                def pv():
                    nc.tensor.matmul(op_.ap[:, 0:129], lhsT=pe_.ap[:, 0:128], rhs=Vp.ap[:, pblk, 0:129], start=True, stop=False)
                    return nc.tensor.matmul(op_.ap[:, 0:129], lhsT=pe_.ap[:, 128:256], rhs=V.ap[:, n, 0:129], start=False, stop=True)
                K.op(PE, pv, reads=[pe_, Vp, V], writes=[op_])
                o = Os.next()
                evac(o.ap[:, 0:129], op_.ap[:, 0:129], [op_], [o])
                if g == 0:
                    t0, stp = n * 128, 1
                elif g == 1:
                    r, m = n // 4, n % 4
                    t0, stp = 4 * (128 * m) + r, 4
                else:
                    t0, stp = n, 16
                K.dma(SP, NUM[g, t0:t0 + 127 * stp + 1:stp, j, 0:129], o.ap[:, 0:129], o, reads=[o])

            pend = []
            for g in range(3):
                for j in range(8):
                    tl = (QTs.next(), KTs.next(), KHs.next(), Vs.next(), VHs.next())
                    QT, KT, KH, V, VH = tl
                    rows = slice(j * 128, (j + 1) * 128)
                    K.dma(SP, QT.ap[:], AQT[g, rows, :], QT, writes=[QT])
                    K.dma(SP, KT.ap[:], AKT[g, rows, :], KT, writes=[KT])
                    K.dma(SP, KH.ap[:], AKH[g, rows, :], KH, writes=[KH])
                    K.dma(SP, V.ap[:, :, 0:128], AV[g, :, rows].rearrange("(b p) e -> p b e", p=128), V, writes=[V], part=True)
                    K.dma(SP, VH.ap[:, :, 0:128], AVH[g, :, rows].rearrange("(b p) e -> p b e", p=128), VH, writes=[VH], part=True)
                    for n in range(16):
                        pend.append(stageP(g, j, n, tl))
                        if len(pend) > 2:
                            stageQ(pend.pop(0))
            while pend:
                stageQ(pend.pop(0))
        with Phase(K) as ph:
            ns = ph.sbs(2, [128, 3, 8, 132], F32, "ns")
            sm = ph.sbs(2, [128, 8, 132], F32, "sm")
            rc = ph.sbs(2, [128, 8, 1], F32, "rc")
            at = ph.sbs(2, [128, 8, 128], BF16, "at")
            stg = ph.sbs(2, [128, 8, 512], BF16, "astg")
            pts = ph.pss(2, [128, 512], BF16, "apt")
            st = None
            for i in range(16):
                n_ = ns.next()
                for g in range(3):
                    K.dma(SP, n_.ap[:, g, :, :], NUM[g, i * 128:(i + 1) * 128, :, :], n_, writes=[n_], part=(g > 0))
                s_ = sm.next()
                K.op(DVE, lambda: nc.vector.tensor_tensor(out=s_.ap[:], in0=n_.ap[:, 0], in1=n_.ap[:, 1], op=ALU.add), reads=[n_], writes=[s_])
                K.op(DVE, lambda: nc.vector.tensor_tensor(out=s_.ap[:], in0=s_.ap[:], in1=n_.ap[:, 2], op=ALU.add), reads=[n_, s_], writes=[s_])
                r_ = rc.next()
                K.op(DVE, lambda: nc.vector.reciprocal(out=r_.ap[:], in_=s_.ap[:, :, 128:129]), reads=[s_], writes=[r_])
                a_ = at.next()
                K.op(DVE, lambda: nc.vector.tensor_tensor(out=a_.ap[:], in0=s_.ap[:, :, 0:128], in1=r_.ap[:].broadcast_to([128, 8, 128]),
                                                          op=ALU.mult), reads=[s_, r_], writes=[a_])
                if i % 4 == 0:
                    st = stg.next()
                for hh in range(2):
                    pt = pts.next()

                    def tr():
                        ins = None
                        for jj in range(4):
                            ins = nc.tensor.transpose(out=pt.ap[:, jj * 128:(jj + 1) * 128], in_=a_.ap[:, hh * 4 + jj, :], identity=identb.ap[:])
                        return ins
                    K.op(PE, tr, reads=[a_, identb], writes=[pt])
                    evac(st.ap[:, hh * 4:(hh + 1) * 4, (i % 4) * 128:(i % 4 + 1) * 128], pt.ap[:, :].rearrange("p (a b) -> p a b", a=4),
                         [pt], [st], part=True)
                if i % 4 == 3:
                    K.dma(SP, ATT.rearrange("(j p) t -> p j t", p=128)[:, :, (i - 3) * 128:(i + 1) * 128], st.ap[:], st, reads=[st])

    CS_d = dscr("CS", [8, 128, 1024], F32)
    NS_d = dscr("NS", [8, 128, 2], F32)
    Cst = [None] * 8
    nst = [None] * 8
    FIRST_OWN = OWN0 // 128

    def alloc_state(ph, zero):
        for h in range(8):
            Cst[h] = ph.sb([128, 2, 512], F32, "Cst")
            nst[h] = ph.sb([128, 2], F32, "nst")
            if zero:
                K.op(DVE, lambda h=h: nc.vector.memset(Cst[h].ap[:], 0.0), writes=[Cst[h]])
                K.op(DVE, lambda h=h: nc.vector.memset(nst[h].ap[:], 0.0), writes=[nst[h]])
            else:
                K.dma(SP, Cst[h].ap[:].rearrange("p a b -> p (a b)"), CS_d[h, :, :], Cst[h], writes=[Cst[h]])
                K.dma(SP, nst[h].ap[:], NS_d[h, :, :], nst[h], writes=[nst[h]])

    def chunk_gates(gp, if_, w_, wb_, eb_, dc_, tm_):
        def gm():
            nc.tensor.matmul(gp.ap[:, 0:8], lhsT=U_f, rhs=if_.ap[:, 8:16], start=True, stop=True)
            return nc.tensor.matmul(gp.ap[:, 8:16], lhsT=ones_f, rhs=if_.ap[:, 8:16], start=True, stop=True)
        K.op(PE, gm, reads=[cst, if_], writes=[gp])
        K.op(DVE, lambda: nc.vector.tensor_tensor(out=tm_.ap[:], in0=if_.ap[:, 0:8], in1=gp.ap[:, 0:8], op=ALU.subtract),
             reads=[if_, gp], writes=[tm_])
        K.op(ACT, lambda: nc.scalar.activation(out=w_.ap[:], in_=tm_.ap[:], func=AF.Exp), reads=[tm_], writes=[w_])
        if eb_ is not None:
            K.op(ACT, lambda: nc.scalar.activation(out=eb_.ap[:], in_=gp.ap[:, 0:8], func=AF.Exp), reads=[gp], writes=[eb_])
        K.op(ACT, lambda: nc.scalar.activation(out=dc_.ap[:], in_=gp.ap[:, 8:16], func=AF.Exp), reads=[gp], writes=[dc_])
        K.op(DVE, lambda: nc.vector.tensor_copy(out=wb_.ap[:], in_=w_.ap[:]), reads=[w_], writes=[wb_])

    def state_update(h, kt, vs, wb_, dc_, pd, dp2, dd, nt_, Cb=None, nb=None):
        def dm():
            for ec in range(2):
                nc.tensor.matmul(pd.ap[:, ec, :], lhsT=kt.ap[:, h * 256 + ec * 128:h * 256 + (ec + 1) * 128], rhs=vs.ap[:], start=True, stop=True)
            ins = None
            for ec in range(2):
                ins = nc.tensor.matmul(dp2.ap[:, 1 + ec:2 + ec], lhsT=kt.ap[:, h * 256 + ec * 128:h * 256 + (ec + 1) * 128],
                                       rhs=wb_.ap[:, h:h + 1], start=True, stop=True)
            return ins
        K.op(PE, dm, reads=[kt, vs, wb_], writes=[pd, dp2])
        K.op(ACT, lambda: nc.scalar.activation(out=dd.ap[:], in_=pd.ap[:], func=AF.Copy, scale=dc_.ap[:, h:h + 1]),
             reads=[pd, dc_], writes=[dd])
        K.op(DVE, lambda: nc.vector.scalar_tensor_tensor(out=Cst[h].ap[:], in0=Cst[h].ap[:], scalar=dc_.ap[:, h:h + 1], in1=dd.ap[:],
                                                         op0=ALU.mult, op1=ALU.add), reads=[Cst[h], dc_, dd], writes=[Cst[h]])
        if Cb is not None:
            K.op(ACT, lambda: nc.scalar.copy(out=Cb[h].ap[:], in_=Cst[h].ap[:]), reads=[Cst[h]], writes=[Cb[h]])
        K.op(DVE, lambda: nc.vector.tensor_tensor(out=nt_.ap[:], in0=nst[h].ap[:], in1=dp2.ap[:, 1:3], op=ALU.add),
             reads=[nst[h], dp2], writes=[nt_])
        K.op(DVE, lambda: nc.vector.tensor_scalar(out=nst[h].ap[:], in0=nt_.ap[:], scalar1=dc_.ap[:, h:h + 1], scalar2=None, op0=ALU.mult),
             reads=[nt_, dc_], writes=[nst[h]])
        if nb is not None:
            K.op(DVE, lambda: nc.vector.tensor_copy(out=nb[h].ap[:], in_=nst[h].ap[:]), reads=[nst[h]], writes=[nb[h]])

    def mlstm_prefix(ph):
        kts = ph.sbs(2, [128, 2048], BF16, "pktm")
        vts = ph.sbs(2, [128, 4096], BF16, "pvtm")
        ifs = ph.sbs(2, [128, 16], F32, "pifs")
        gw = ph.sbs(2, [128, 8], F32, "pgw")
        gwb = ph.sbs(2, [128, 8], BF16, "pgwb")
        gdc = ph.sbs(2, [128, 8], F32, "pgdc")
        gtmp = ph.sbs(2, [128, 8], F32, "pgtmp")
        vsc = ph.sbs(2, [128, 512], BF16, "pvsc")
        dCd = ph.sbs(1, [128, 2, 512], F32, "pdCd")
        ntmp = ph.sbs(2, [128, 2], F32, "pntmp")
        p_dC = ph.ps([128, 2, 512], F32, "ppdC")
        p_sm = ph.ps([128, 512], F32, "ppsm")
        gate_ps = Rot([T(p_sm.ap[:, i * 16:(i + 1) * 16], p_sm.buf) for i in range(4)])
        den_ps = Rot([T(p_sm.ap[:, 64 + i * 4:64 + (i + 1) * 4], p_sm.buf) for i in range(8)])

        alloc_state(ph, True)

        def gen():
            for c in range(FIRST_OWN):
                e0 = c * 128
                kt, vt, if_ = kts.next(), vts.next(), ifs.next()
                K.dma(SP, kt.ap[:], MK[e0:e0 + 128, :], kt, writes=[kt])
                K.dma(SP, vt.ap[:], MV[e0:e0 + 128, :], vt, writes=[vt])
                K.dma(SP, if_.ap[:], MIF[e0:e0 + 128, :], if_, writes=[if_])
                w_, wb_, dc_, tm_ = gw.next(), gwb.next(), gdc.next(), gtmp.next()
                chunk_gates(gate_ps.next(), if_, w_, wb_, None, dc_, tm_)
                yield
                for h in range(8):
                    vs = vsc.next()
                    K.op(ACT, lambda: nc.scalar.activation(out=vs.ap[:], in_=vt.ap[:, h * 512:(h + 1) * 512], func=AF.Copy, scale=w_.ap[:, h:h + 1]),
                         reads=[vt, w_], writes=[vs])
                    state_update(h, kt, vs, wb_, dc_, p_dC, den_ps.next(), dCd.next(), ntmp.next())
                    yield
            for h in range(8):
                K.dma(SP, CS_d[h, :, :], Cst[h].ap[:].rearrange("p a b -> p (a b)"), Cst[h], reads=[Cst[h]])
                K.dma(SP, NS_d[h, :, :], nst[h].ap[:], nst[h], reads=[nst[h]])
        return gen()

    XN = dscr("XN", [OWN0, D], BF16)

    def prefix_state():
        NCH = FIRST_OWN
        with Phase(K) as ph:
            mif = ph.sb([128, NCH, 16], F32, "pmif")
            K.dma(SP, mif.ap[:], MIF.rearrange("(c p) g -> p c g", p=128)[:, 0:NCH, :], mif, writes=[mif])
            Lx = ph.sb([128, 128], F32, "Lx")
            K.op(DVE, lambda: nc.vector.tensor_tensor(out=Lx.ap[:], in0=ones_f, in1=U_f, op=ALU.subtract), reads=[cst], writes=[Lx])
            onesb = ph.sb([128, 2], BF16, "onesb")
            K.op(DVE, lambda: nc.vector.tensor_copy(out=onesb.ap[:], in_=cst.ap[:, 256:258]), reads=[cst], writes=[onesb])
            Wl = ph.sb([128, NCH, 8], F32, "Wl")
            carry = ph.sb([128, 8], F32, "carry")
            K.op(DVE, lambda: nc.vector.memset(carry.ap[:], 0.0), writes=[carry])
            p_g = ph.ps([128, 512], F32, "ppg")
            gate_ps = Rot([T(p_g.ap[:, i * 16:(i + 1) * 16], p_g.buf) for i in range(4)])
            n_ps = [T(p_g.ap[:, 64 + i:65 + i], p_g.buf) for i in range(4)]
            p_G = [ph.ps([128, 512], F32, "ppG") for _ in range(4)]
            p_C = ph.pss(2, [128, 512], F32, "ppC")
            for c in range(NCH - 1, -1, -1):
                gp = gate_ps.next()

                def gm():
                    nc.tensor.matmul(gp.ap[:, 0:8], lhsT=Lx.ap[:], rhs=mif.ap[:, c, 8:16], start=True, stop=True)
                    return nc.tensor.matmul(gp.ap[:, 8:16], lhsT=ones_f, rhs=mif.ap[:, c, 8:16], start=True, stop=True)
                K.op(PE, gm, reads=[Lx, cst, mif], writes=[gp])
                K.op(DVE, lambda: nc.vector.tensor_tensor(out=Wl.ap[:, c, :], in0=gp.ap[:, 0:8], in1=carry.ap[:], op=ALU.add),
                     reads=[gp, carry], writes=[Wl], part=True)
                K.op(DVE, lambda: nc.vector.tensor_tensor(out=Wl.ap[:, c, :], in0=Wl.ap[:, c, :], in1=mif.ap[:, c, 0:8], op=ALU.add),
                     reads=[Wl, mif], writes=[Wl], part=True)
                K.op(DVE, lambda: nc.vector.tensor_tensor(out=carry.ap[:], in0=carry.ap[:], in1=gp.ap[:, 8:16], op=ALU.add),
                     reads=[carry, gp], writes=[carry])
            K.op(ACT, lambda: nc.scalar.activation(out=Wl.ap[:].rearrange("p c h -> p (c h)"), in_=Wl.ap[:].rearrange("p c h -> p (c h)"), func=AF.Exp),
                 reads=[Wl], writes=[Wl])
            kqs = ph.sbs(2, [128, NCH, 512], BF16, "kq")
            GTq = ph.sb([128, 32, 512], BF16, "GTq")
            xns = ph.sbs(5, [128, 4, 512], BF16, "pxn")
            wvs = ph.sbs(2, [128, 16, 512], BF16, "pwv")
            css = ph.sbs(2, [128, 512], F32, "pcs")
            nss = ph.sbs(2, [128, 4], F32, "pns")
            mkv = MK.rearrange("(c p) n -> p c n", p=128)
            wv_in = w_in.rearrange("(kc p) n -> p kc n", p=128)
            def load_kq(q):
                kq = kqs.next()
                for hh in range(2):
                    K.dma(SP, kq.ap[:, hh * 24:(hh + 1) * 24, :], mkv[:, hh * 24:(hh + 1) * 24, q * 512:(q + 1) * 512], kq, writes=[kq], part=(hh > 0))
                for c in range(NCH):
                    K.op(DVE, lambda: nc.vector.tensor_tensor(out=kq.ap[:, c, :].rearrange("p (h e) -> p h e", h=2),
                                                              in0=kq.ap[:, c, :].rearrange("p (h e) -> p h e", h=2),
                                                              in1=Wl.ap[:, c, 2 * q:2 * q + 2].unsqueeze(2).broadcast_to([128, 2, 256]), op=ALU.mult),
                         reads=[kq, Wl], writes=[kq], part=True)
                return kq
            kq_next = load_kq(0)
            for q in range(4):
                kq = kq_next
                def load_wv(hl):
                    c0 = C_MV + (2 * q + hl) * 512
                    res = []
                    for hh in range(2):
                        wv = wvs.next()
                        K.dma(POOL, wv.ap[:], wv_in[:, hh * 16:(hh + 1) * 16, c0:c0 + 512], wv, writes=[wv])
                        res.append(wv)
                    return res
                wvh = load_wv(0)
                if q + 1 < 4:
                    kq_next = load_kq(q + 1)
                def nmm():
                    ins = None
                    for hl in range(2):
                        for ec in range(2):
                            o = n_ps[hl * 2 + ec]
                            for c in range(NCH):
                                ins = nc.tensor.matmul(o.ap, lhsT=kq.ap[:, c, hl * 256 + ec * 128:hl * 256 + (ec + 1) * 128], rhs=onesb.ap[:, 0:1],
                                                       start=(c == 0), stop=(c == NCH - 1))
                    return ins
                K.op(PE, nmm, reads=[kq, onesb], writes=[n_ps[0]])
                ns_ = nss.next()
                K.op(DVE, lambda: nc.vector.tensor_copy(out=ns_.ap[:], in_=p_g.ap[:, 64:68]), reads=[n_ps[0]], writes=[ns_])
                for hl in range(2):
                    K.dma(SP, NS_d[2 * q + hl, :, :], ns_.ap[:, hl * 2:hl * 2 + 2], ns_, reads=[ns_])
                xnv = XN.rearrange("(c p) f -> p c f", p=128)
                for fbg in range(8):
                    for cg in range(NCH // 4):
                        xn = xns.next()
                        K.dma(SP if (cg % 2 == 0) else ACT, xn.ap[:], xnv[:, cg * 4:(cg + 1) * 4, fbg * 512:(fbg + 1) * 512], xn, writes=[xn])

                        def gmm():
                            ins = None
                            for j in range(4):
                                c = cg * 4 + j
                                for fl in range(4):
                                    ins = nc.tensor.matmul(p_G[fl].ap[:, :], lhsT=xn.ap[:, j, fl * 128:(fl + 1) * 128], rhs=kq.ap[:, c, :],
                                                           start=(c == 0), stop=(c == NCH - 1))
                            return ins
                        K.op(PE, gmm, reads=[xn, kq], writes=p_G, part=(cg > 0))
                    for fl in range(4):
                        evac(GTq.ap[:, fbg * 4 + fl, :], p_G[fl].ap[:, :], [p_G[fl]], [GTq], part=True)
                for hl in range(2):
                    h = 2 * q + hl
                    if hl == 1:
                        wvh = load_wv(1)
                    for ec in range(2):
                        pc = p_C.next()

                        def cmm():
                            ins = None
                            for kc in range(32):
                                ins = nc.tensor.matmul(pc.ap[:, :], lhsT=GTq.ap[:, kc, hl * 256 + ec * 128:hl * 256 + (ec + 1) * 128],
                                                       rhs=wvh[kc // 16].ap[:, kc % 16, :], start=(kc == 0), stop=(kc == 31))
                            return ins
                        K.op(PE, cmm, reads=[GTq] + wvh, writes=[pc])
                        cs_ = css.next()
                        evac(cs_.ap[:], pc.ap[:, :], [pc], [cs_])
                        K.dma(SP, CS_d[h, :, ec * 512:(ec + 1) * 512], cs_.ap[:], cs_, reads=[cs_])

    def mlstm_own():
        with Phase(K) as ph:
            ghb = ph.sb([128, D], F32, "ghb")
            K.dma(SP, ghb.ap[:], g_h.partition_broadcast(128), ghb, writes=[ghb])
            alloc_state(ph, False)
            Cb = [ph.sb([128, 2, 512], BF16, "Cb") for _ in range(8)]
            nb = [ph.sb([128, 2], BF16, "nb") for _ in range(8)]
            for h in range(8):
                K.op(ACT, lambda h=h: nc.scalar.copy(out=Cb[h].ap[:], in_=Cst[h].ap[:]), reads=[Cst[h]], writes=[Cb[h]])
                K.op(DVE, lambda h=h: nc.vector.tensor_copy(out=nb[h].ap[:], in_=nst[h].ap[:]), reads=[nst[h]], writes=[nb[h]])
            kts = ph.sbs(2, [128, 2048], BF16, "ktm")
            vts = ph.sbs(2, [128, 4096], BF16, "vtm")
            ots = ph.sbs(2, [128, 4096], BF16, "otm")
            ifs = ph.sbs(2, [128, 16], F32, "ifs")
            qTs = ph.sbs(2, [128, 16, 256], BF16, "qT")
            kTs = ph.sbs(2, [128, 16, 128], BF16, "kT")
            gw = ph.sbs(2, [128, 8], F32, "gw")
            gwb = ph.sbs(2, [128, 8], BF16, "gwb")
            geb = ph.sbs(2, [128, 8], F32, "geb")
            gdc = ph.sbs(2, [128, 8], F32, "gdc")
            gtmp = ph.sbs(2, [128, 8], F32, "gtmp")
            vsc = ph.sbs(4, [128, 512], BF16, "vsc")
            dCd = ph.sbs(1, [128, 2, 512], F32, "dCd")
            ATs = ph.sbs(3, [128, 128], BF16, "ATs")
            sml = ph.sbs(8, [128, 8], F32, "sml")
            ntmp = ph.sbs(2, [128, 2], F32, "ntmp")
            hts = ph.sbs(4, [128, 512], F32, "hts")
            jnk = ph.sb([128, 512], BF16, "mjnk")
            mls = ph.sbs(2, [128, 4096], BF16, "mls")
            mstg = ph.sb([128, 32, 512], BF16, "mstg")
            p_dC = ph.pss(1, [128, 2, 512], F32, "pdC")
            p_AT = ph.ps([128, 512], F32, "pAT")
            p_nums = ph.pss(2, [128, 512], F32, "pnum")
            p_sm = ph.ps([128, 512], F32, "psm")
            p_den = ph.ps([128, 512], F32, "pden")
            p_tr_ = ph.ps([128, 1024], BF16, "ptr")
            p_tr = Rot([T(p_tr_.ap[:, i * 512:(i + 1) * 512], p_tr_.buf) for i in range(2)])
            at_rot = Rot([T(p_AT.ap[:, i * 128:(i + 1) * 128], p_AT.buf) for i in range(4)])
            gate_ps = Rot([T(p_sm.ap[:, i * 16:(i + 1) * 16], p_sm.buf) for i in range(4)])
            dn_rot = Rot([T(p_sm.ap[:, 64 + i * 4:64 + (i + 1) * 4], p_sm.buf) for i in range(8)])
            den_rot = Rot([T(p_den.ap[:, i * 4:(i + 1) * 4], p_den.buf) for i in range(8)])
            def make_chunk(oc):
                e0 = OWN0 + oc * 128
                cx = dict(oc=oc)
                kt, vt, if_ = kts.next(), vts.next(), ifs.next()
                K.dma(SP, kt.ap[:], MK[e0:e0 + 128, :], kt, writes=[kt])
                K.dma(SP, vt.ap[:], MV[e0:e0 + 128, :], vt, writes=[vt])
                K.dma(SP, if_.ap[:], MIF[e0:e0 + 128, :], if_, writes=[if_])
                ot = ots.next()
                K.dma(SP, ot.ap[:], MO[oc * 128:(oc + 1) * 128, :], ot, writes=[ot])
                if oc % 2 == 0:
                    qk["q"] = qTs.next()
                    K.dma(SP, qk["q"].ap[:], MQT.rearrange("(a p) t -> p a t", p=128)[:, :, oc * 128:oc * 128 + 256], qk["q"], writes=[qk["q"]])
                kTc = kTs.next()
                for q4 in range(4):
                    pt = p_tr.next()

                    def trk():
                        ins = None
                        for jj in range(4):
                            a_ = q4 * 4 + jj
                            ins = nc.tensor.transpose(out=pt.ap[:, jj * 128:(jj + 1) * 128], in_=kt.ap[:, a_ * 128:(a_ + 1) * 128], identity=identb.ap[:])
                        return ins
                    K.op(PE, trk, reads=[kt, identb], writes=[pt])
                    evac(kTc.ap[:, q4 * 4:(q4 + 1) * 4, :], pt.ap[:, :].rearrange("p (a b) -> p a b", a=4), [pt], [kTc], part=True)
                qk["k"] = kTc
                w_, wb_, eb_, dc_, tm_ = gw.next(), gwb.next(), geb.next(), gdc.next(), gtmp.next()
                chunk_gates(gate_ps.next(), if_, w_, wb_, eb_, dc_, tm_)
                cx.update(kt=kt, vt=vt, ot=ot, qT=qk["q"], kT=qk["k"], w_=w_, wb_=wb_, eb_=eb_, dc_=dc_, ml=mls.next(),
                          tcs=slice((oc % 2) * 128, (oc % 2) * 128 + 128))
                return cx

            def stA(it):
                cx, h = it["cx"], it["h"]
                qT, kT, tcs, vt, w_, wb_ = cx["qT"], cx["kT"], cx["tcs"], cx["vt"], cx["w_"], cx["wb_"]
                vs = vsc.next()
                it["vs"] = vs
                K.op(ACT, lambda: nc.scalar.activation(out=vs.ap[:], in_=vt.ap[:, h * 512:(h + 1) * 512], func=AF.Copy, scale=w_.ap[:, h:h + 1]),
                     reads=[vt, w_], writes=[vs])
                ap_ = at_rot.next()

                def am():
                    ins = None
                    for ec in range(2):
                        ins = nc.tensor.matmul(ap_.ap, lhsT=kT.ap[:, h * 2 + ec, :], rhs=qT.ap[:, h * 2 + ec, tcs],
                                               start=(ec == 0), stop=(ec == 1))
                    return ins
                K.op(PE, am, reads=[kT, qT], writes=[ap_])
                as_ = ATs.next()
                K.op(DVE, lambda: nc.vector.tensor_tensor(out=as_.ap[:], in0=ap_.ap, in1=U_f, op=ALU.mult), reads=[ap_, cst], writes=[as_])
                dp = den_rot.next()
                pn = p_nums.next()
                it["dp"], it["pn"] = dp, pn

                def nm():
                    for ec in range(2):
                        nc.tensor.matmul(pn.ap[:, :], lhsT=qT.ap[:, h * 2 + ec, tcs], rhs=Cb[h].ap[:, ec, :], start=(ec == 0), stop=False)
                    nc.tensor.matmul(pn.ap[:, :], lhsT=as_.ap[:], rhs=vs.ap[:], start=False, stop=True)
                    for ec in range(2):
                        nc.tensor.matmul(dp.ap[:, 0:1], lhsT=qT.ap[:, h * 2 + ec, tcs], rhs=nb[h].ap[:, ec:ec + 1], start=(ec == 0), stop=False)
                    return nc.tensor.matmul(dp.ap[:, 0:1], lhsT=as_.ap[:], rhs=wb_.ap[:, h:h + 1], start=False, stop=True)
                K.op(PE, nm, reads=[qT, Cb[h], nb[h], as_, vs, wb_], writes=[pn, dp])

            def stB(it):
                cx, h, dp, pn = it["cx"], it["h"], it["dp"], it["pn"]
                eb_ = cx["eb_"]
                s4 = sml.next()
                ht = hts.next()
                it["s4"], it["ht"] = s4, ht
                K.op(DVE, lambda: nc.vector.tensor_tensor(out=s4.ap[:, 0:1], in0=dp.ap[:, 0:1], in1=eb_.ap[:, h:h + 1], op=ALU.mult),
                     reads=[dp, eb_], writes=[s4])
                K.op(DVE, lambda: nc.vector.tensor_scalar(out=s4.ap[:, 1:2], in0=s4.ap[:, 0:1], scalar1=1.0, scalar2=None, op0=ALU.max),
                     reads=[s4], writes=[s4])
                K.op(DVE, lambda: nc.vector.tensor_scalar(out=s4.ap[:, 2:3], in0=s4.ap[:, 0:1], scalar1=-1.0, scalar2=1.0, op0=ALU.mult, op1=ALU.max),
                     reads=[s4], writes=[s4])
                K.op(DVE, lambda: nc.vector.tensor_tensor(out=s4.ap[:, 3:4], in0=s4.ap[:, 1:2], in1=s4.ap[:, 2:3], op=ALU.max),
                     reads=[s4], writes=[s4])
                K.op(DVE, lambda: nc.vector.reciprocal(out=s4.ap[:, 3:4], in_=s4.ap[:, 3:4]), reads=[s4], writes=[s4])
                K.op(DVE, lambda: nc.vector.tensor_tensor(out=s4.ap[:, 4:5], in0=eb_.ap[:, h:h + 1], in1=s4.ap[:, 3:4], op=ALU.mult),
                     reads=[s4, eb_], writes=[s4])
                K.op(ACT, lambda: nc.scalar.activation(out=jnk.ap[:], in_=pn.ap[:, :], func=AF.Square, scale=s4.ap[:, 4:5],
                                                       accum_out=s4.ap[:, 5:6]), reads=[pn, s4], writes=[jnk, s4])
                K.op(DVE, lambda: nc.vector.scalar_tensor_tensor(out=ht.ap[:], in0=pn.ap[:, :], scalar=s4.ap[:, 4:5],
                                                                 in1=ghb.ap[:, h * 512:(h + 1) * 512], op0=ALU.mult, op1=ALU.mult),
                     reads=[pn, s4, ghb], writes=[ht])
                K.op(DVE, lambda: nc.vector.tensor_scalar(out=s4.ap[:, 6:7], in0=s4.ap[:, 5:6], scalar1=1.0 / 512.0, scalar2=EPS,
                                                          op0=ALU.mult, op1=ALU.add), reads=[s4], writes=[s4])

            def stC(it):
                s4 = it["s4"]
                K.op(ACT, lambda: nc.scalar.activation(out=s4.ap[:, 6:7], in_=s4.ap[:, 6:7], func=AF.Sqrt), reads=[s4], writes=[s4])

            def stD(it):
                cx, h, s4, ht = it["cx"], it["h"], it["s4"], it["ht"]
                ml, ot = cx["ml"], cx["ot"]
                K.op(DVE, lambda: nc.vector.reciprocal(out=s4.ap[:, 6:7], in_=s4.ap[:, 6:7]), reads=[s4], writes=[s4])
                K.op(DVE, lambda: nc.vector.scalar_tensor_tensor(out=ml.ap[:, h * 512:(h + 1) * 512], in0=ht.ap[:], scalar=s4.ap[:, 6:7],
                                                                 in1=ot.ap[:, h * 512:(h + 1) * 512], op0=ALU.mult, op1=ALU.mult),
                     reads=[ht, s4, ot], writes=[ml], part=True)
                if h == 7:
                    finish_chunk(cx)

            def stE(it):
                cx, h = it["cx"], it["h"]
                state_update(h, cx["kt"], it["vs"], cx["wb_"], cx["dc_"], p_dC.next(), dn_rot.next(), dCd.next(), ntmp.next(), Cb, nb)

            def finish_chunk(cx):
                oc, ml = cx["oc"], cx["ml"]
                sub = oc % 4
                for q4 in range(8):
                    pt = p_tr.next()

                    def tr():
                        ins = None
                        for jj in range(4):
                            kc = q4 * 4 + jj
                            ins = nc.tensor.transpose(out=pt.ap[:, jj * 128:(jj + 1) * 128], in_=ml.ap[:, kc * 128:(kc + 1) * 128], identity=identb.ap[:])
                        return ins
                    K.op(PE, tr, reads=[ml, identb], writes=[pt])
                    evac(mstg.ap[:, q4 * 4:(q4 + 1) * 4, sub * 128:(sub + 1) * 128], pt.ap[:, :].rearrange("p (a b) -> p a b", a=4),
                         [pt], [mstg], part=True)
                if sub == 3:
                    t0 = (oc - 3) * 128
                    mv = MT.rearrange("(kc p) t -> p kc t", p=128)
                    for hh in range(2):
                        K.dma(SP, mv[:, hh * 16:(hh + 1) * 16, t0:t0 + 512], mstg.ap[:, hh * 16:(hh + 1) * 16, :], mstg, reads=[mstg])

            qk = {}
            items = []
            nit = (NOWN // 128) * 8
            stages = ((stA, 0), (stB, 1), (stE, 1), (stC, 2), (stD, 3))
            for t in range(nit + 3):
                if t < nit and t % 8 == 0:
                    cxc = make_chunk(t // 8)
                if t < nit:
                    items.append(dict(cx=cxc, h=t % 8))
                for fn, lag in stages:
                    k = t - lag
                    if 0 <= k < nit:
                        fn(items[k])

    MGT = dscr("MGT", [D, NOWN], BF16)

    def merge_tile(ti):
        o0 = ti * 1024
        with Phase(K) as ph:
            G = Gemm(ph, 32, 1024)
            wa = ph.sbs(2, [128, 8, 512], BF16, "wa")
            ATa = ph.sb([128, 8, 1024], BF16, "ATa")
            ATm = ph.sb([128, 32, 1024], BF16, "ATm")
            K.dma(SP, ATa.ap[:], ATT.rearrange("(kc p) t -> p kc t", p=128)[:, :, o0:o0 + 1024], ATa, writes=[ATa])
            mv = MT.rearrange("(kc p) t -> p kc t", p=128)
            for h in range(4):
                K.dma(SP, ATm.ap[:, h * 8:(h + 1) * 8, :], mv[:, h * 8:(h + 1) * 8, o0:o0 + 1024], ATm, writes=[ATm], part=(h > 0))
            gas = ph.sbs(8, [128, 1024], BF16, "ga")
            gms = ph.sbs(8, [128, 1024], BF16, "gm")
            waq = []
            t1s = ph.sbs(2, [128, 512], F32, "t1")
            t2s = ph.sbs(2, [128, 512], F32, "t2")
            mgs = ph.sbs(3, [128, 1024], BF16, "mgs")
            panels = []
            for pb in range(8):
                def pre(pb=pb):
                    wap = wa.next()
                    K.dma(POOL, wap.ap[:], w_ab.rearrange("(kc p) n -> p kc n", p=128)[:, :, pb * 512:(pb + 1) * 512], wap, writes=[wap])
                    gl = []
                    for cb in range(4):
                        cg = pb * 4 + cb
                        ga, gm_ = gas.next(), gms.next()
                        K.dma(SP, ga.ap[:], GT[cg * 128:(cg + 1) * 128, o0:o0 + 1024], ga, writes=[ga])
                        K.dma(SP, gm_.ap[:], GT[D + cg * 128:D + (cg + 1) * 128, o0:o0 + 1024], gm_, writes=[gm_])
                        gl.append((ga, gm_))
                    waq.append((wap, gl))

                def jobs(wp, pb=pb):
                    wap, gl = waq.pop(0)
                    for cb in range(4):
                        cg = pb * 4 + cb
                        ga, gm_ = gl[cb]
                        mg = mgs.next()
                        for tg in range(2):
                            tsl = slice(tg * 512, (tg + 1) * 512)
                            psm = G.group([(lambda p: p.ap[:, :], wp.ap[:, kc, cb * 128:(cb + 1) * 128], ATm.ap[:, kc, tsl]) for kc in range(32)], [ATm, wp])
                            psa = G.group([(lambda p: p.ap[:, :], wap.ap[:, kc, cb * 128:(cb + 1) * 128], ATa.ap[:, kc, tsl]) for kc in range(8)], [ATa, wap])
                            t1, t2 = t1s.next(), t2s.next()
                            K.op(DVE, lambda: nc.vector.tensor_tensor(out=t1.ap[:], in0=psm.ap[:, :], in1=gm_.ap[:, tsl], op=ALU.mult), reads=[psm, gm_], writes=[t1])
                            K.op(DVE, lambda: nc.vector.tensor_tensor(out=t2.ap[:], in0=psa.ap[:, :], in1=ga.ap[:, tsl], op=ALU.mult), reads=[psa, ga], writes=[t2])
                            K.op(DVE, lambda: nc.vector.tensor_tensor(out=mg.ap[:, tsl], in0=t1.ap[:], in1=t2.ap[:], op=ALU.add),
                                 reads=[t1, t2], writes=[mg], part=True)
                        K.dma(SP, MGT[cg * 128:(cg + 1) * 128, o0:o0 + 1024], mg.ap[:], mg, reads=[mg])
                panels.append(dict(w=w_mb, c0=pb * 512, ncols=512, jobs=jobs, pre=pre))
            G.run(panels)

    def outproj_tile(ti):
        o0 = ti * 1024
        with Phase(K) as ph:
            G = Gemm(ph, 32, 1024)
            MG = ph.sb([128, 32, 1024], BF16, "MG")
            gv = MGT.rearrange("(kc p) t -> p kc t", p=128)
            for h in range(4):
                K.dma(SP, MG.ap[:, h * 8:(h + 1) * 8, :], gv[:, h * 8:(h + 1) * 8, o0:o0 + 1024], MG, writes=[MG], part=(h > 0))
            xr = ph.sbs(4, [128, 512], F32, "xr")
            hs = ph.sbs(4, [128, 512], F32, "hs")
            panels = []
            for pb in range(8):
                def jobs(wp, pb=pb):
                    for ts in range(8):
                        x_ = xr.next()
                        r0 = o0 + ts * 128
                        K.dma(SP, x_.ap[:], x_ext[OWN0 + r0:OWN0 + r0 + 128, pb * 512:(pb + 1) * 512], x_, writes=[x_])
                        ps = G.group([(lambda p: p.ap[:, :], MG.ap[:, kc, ts * 128:(ts + 1) * 128], wp.ap[:, kc, :]) for kc in range(32)], [MG, wp])
                        h_ = hs.next()
                        K.op(DVE, lambda: nc.vector.tensor_tensor(out=h_.ap[:], in0=ps.ap[:, :], in1=x_.ap[:], op=ALU.add), reads=[ps, x_], writes=[h_])
                        K.dma(SP, H1[r0:r0 + 128, pb * 512:(pb + 1) * 512], h_.ap[:], h_, reads=[h_])
                panels.append(dict(w=w_out, c0=pb * 512, ncols=512, jobs=jobs))
            G.run(panels)

    def up_tile(ti):
        o0 = ti * 1024
        with Phase(K) as ph:
            G = Gemm(ph, 32, 1024)
            AT = ph.sb([128, 32, 1024], BF16, "ATu")
            hv_ = HT.rearrange("(kc p) t -> p kc t", p=128)
            for h in range(4):
                K.dma(SP, AT.ap[:, h * 8:(h + 1) * 8, :], hv_[:, h * 8:(h + 1) * 8, o0:o0 + 1024], AT, writes=[AT], part=(h > 0))
            rl = ph.sbs(3, [128, 512], F32, "rl")
            stg = ph.sbs(3, [128, 1024], BF16, "ustg")
            panels = []
            for pb in range(32):
                def jobs(wp, pb=pb):
                    for cb in range(4):
                        st = stg.next()
                        for tg in range(2):
                            ps = G.group([(lambda p: p.ap[:, :], wp.ap[:, kc, cb * 128:(cb + 1) * 128], AT.ap[:, kc, tg * 512:(tg + 1) * 512])
                                          for kc in range(32)], [AT, wp])
                            r_ = rl.next()
                            K.op(ACT, lambda: nc.scalar.activation(out=r_.ap[:], in_=ps.ap[:, :], func=AF.Relu), reads=[ps], writes=[r_])
                            K.op(DVE, lambda: nc.vector.tensor_tensor(out=st.ap[:, tg * 512:(tg + 1) * 512], in0=r_.ap[:], in1=r_.ap[:], op=ALU.mult),
                                 reads=[r_], writes=[st], part=True)
                        row0 = (pb * 4 + cb) * 128
                        K.dma(SP, UT[row0:row0 + 128, o0:o0 + 1024], st.ap[:], st, reads=[st])
                panels.append(dict(w=w_up, c0=pb * 512, ncols=512, jobs=jobs))
            G.run(panels)

    def down_tile(ti, half):
        o0 = ti * 1024
        k0 = half * 64
        with Phase(K) as ph:
            AT = ph.sb([128, 64, 1024], BF16, "ATd")
            uv = UT.rearrange("(kc p) t -> p kc t", p=128)
            for h in range(8):
                K.dma(SP, AT.ap[:, h * 8:(h + 1) * 8, :], uv[:, k0 + h * 8:k0 + (h + 1) * 8, o0:o0 + 1024], AT, writes=[AT], part=(h > 0))
            wps = ph.sbs(3, [128, 16, 512], BF16, "wd")
            pss = ph.pss(8, [128, 512], F32, "dps")
            hr = ph.sbs(4, [128, 512], F32, "hr")
            hs = ph.sbs(4, [128, 512], F32, "hs2")
            wv = w_down.rearrange("(kc p) n -> p kc n", p=128)
            seq = [(pb, sp) for pb in range(8) for sp in range(4)]
            src = H1 if half == 0 else H2

            def load(i):
                pb, sp = seq[i]
                wp = wps.next()
                K.dma(POOL, wp.ap[:], wv[:, k0 + sp * 16:k0 + (sp + 1) * 16, pb * 512:(pb + 1) * 512], wp, writes=[wp])
                return wp
            nxt = load(0)
            for i, (pb, sp) in enumerate(seq):
                cur = nxt
                if i + 1 < len(seq):
                    nxt = load(i + 1)
                if sp == 0:
                    pst = [pss.next() for _ in range(8)]
                for ts in range(8):
                    ps = pst[ts]

                    def f():
                        ins = None
                        for kk in range(16):
                            kc = sp * 16 + kk
                            ins = nc.tensor.matmul(ps.ap[:, :], lhsT=AT.ap[:, kc, ts * 128:(ts + 1) * 128], rhs=cur.ap[:, kk, :],
                                                   start=(kc == 0), stop=(kc == 63))
                        return ins
                    K.op(PE, f, reads=[AT, cur], writes=[ps], part=(sp > 0))
                if sp == 3:
                    for ts in range(8):
                        r0 = o0 + ts * 128
                        h_ = hr.next()
                        K.dma(SP, h_.ap[:], src[r0:r0 + 128, pb * 512:(pb + 1) * 512], h_, writes=[h_])
                        o_ = hs.next()
                        K.op(DVE, lambda: nc.vector.tensor_tensor(out=o_.ap[:], in0=pst[ts].ap[:, :], in1=h_.ap[:], op=ALU.add),
                             reads=[pst[ts], h_], writes=[o_])
                        K.dma(SP, H2[r0:r0 + 128, pb * 512:(pb + 1) * 512], o_.ap[:], o_, reads=[o_])

    def finish():
        K.barrier()
        return nc

    norm_pass(x_ext, NEXT, g_mix, dstT=XT, dst_tm=XN, ntm=OWN0 // 128)
    if stop_after <= 1:
        return finish()
    proj_phase([0, 1, 2, 3, 4, 5, 6, 7], nat=2)
    prefix_state()
    if stop_after <= 2:
        return finish()
    attention()
    if stop_after <= 3:
        return finish()
    mlstm_own()
    if stop_after <= 4:
        return finish()
    for ti in range(2):
        merge_tile(ti)
    for ti in range(2):
        outproj_tile(ti)
    if stop_after <= 5:
        return finish()
    norm_pass(H1, NOWN, g_mlp, dstT=HT)
    for ti in range(2):
        up_tile(ti)
    if stop_after <= 7:
        return finish()
    for ti in range(2):
        for half in range(2):
            down_tile(ti, half)
    norm_pass(H2, NOWN, g_fin, dst=out_d)
    return finish()


def make_in_maps(inputs):
    x = np.asarray(inputs["x"], np.float32)
    cst, eb = _consts()
    shared = {
        "w_in": np.ascontiguousarray(inputs["w_in"][0], np.float32),
        "w_gate": np.ascontiguousarray(inputs["w_gate"][0], np.float32),
        "w_ab": np.ascontiguousarray(inputs["w_attn_branch"][0], np.float32),
        "w_mb": np.ascontiguousarray(inputs["w_mlstm_branch"][0], np.float32),
        "w_out": np.ascontiguousarray(inputs["w_out"][0], np.float32),
        "w_up": np.ascontiguousarray(inputs["w_up"][0], np.float32),
        "w_down": np.ascontiguousarray(inputs["w_down"][0], np.float32),
        "g_mix": np.ascontiguousarray(inputs["norm_mix_g"][0], np.float32),
        "g_mlp": np.ascontiguousarray(inputs["norm_mlp_g"][0], np.float32),
        "g_fin": np.ascontiguousarray(inputs["norm_final_g"], np.float32),
        "g_h": np.ascontiguousarray(inputs["mlstm_norm_g"][0], np.float32),
        "b_g": np.ascontiguousarray(np.asarray(inputs["b_gate"][0], np.float32).reshape(64, 128).T),
        "b_if": np.ascontiguousarray(np.concatenate([np.asarray(inputs["b_igate"][0], np.float32),
                                                      np.asarray(inputs["b_fgate"][0], np.float32)])),
        "cst": cst,
        "ebias": eb,
    }
    maps = []
    for core in range(8):
        b, c = core // 4, core % 4
        xe = np.zeros((NEXT, D), np.float32)
        n = (c + 1) * NOWN
        xe[NEXT - n:] = x[b, :n]
        m = dict(shared)
        m["x_ext"] = xe
        m["halo_valid"] = np.full((128, 1), 0.0 if c == 0 else 1.0, np.float32)
        maps.append(m)
    return maps


def kernel(**inputs):
    nc = build_program()
    maps = make_in_maps(inputs)
    res = run_bass_kernel_spmd(nc, maps, core_ids=list(range(8)))
    out = np.zeros((2, SEQ, D), np.float32)
    for core in range(8):
        b, c = core // 4, core % 4
        out[b, c * NOWN:(c + 1) * NOWN] = res.results[core]["out"]
    return out
```
